# Optimizing a Trainium2 kernel written in Bass

```python
import jax, jax.numpy as jnp
from jax import lax
import numpy as np

D_MODEL = 2048
BATCH = 2
SEQ = 16384
DEPTH = 4

PLE_DIM = 256
MIX_WIDTH = D_MODEL
HG_KEY = 128
HG_VAL = 128
HG_WIDTH = MIX_WIDTH // 2
HG_HEADS = HG_WIDTH // HG_KEY
ATT_HEAD_DIM = 128
ATT_WIDTH = MIX_WIDTH - HG_WIDTH
ATT_HEADS = ATT_WIDTH // ATT_HEAD_DIM
DILATED_BRANCHES = ((128, 1), (512, 4), (2048, 16))
ATT_BLOCK = 128
CHUNK = 64
D_FF = 5632
CONV_WIDTH = 3
EPS = 1e-6
IN_SIZES = (HG_WIDTH, HG_WIDTH, HG_WIDTH, HG_WIDTH, ATT_WIDTH, ATT_WIDTH, ATT_WIDTH)
IN_COLS = sum(IN_SIZES)

kernel_name = 'hymba_style_hgrn2_dilated_attn_trunk'


def rms_norm(x, w):
    x32 = x.astype(jnp.float32)
    y = x32 * lax.rsqrt(jnp.mean(x32 * x32, axis=-1, keepdims=True) + EPS)
    return (y * w.astype(jnp.float32)).astype(x.dtype)


def hgrn2_mixer(q_raw, f_raw, i_raw, g_raw, lb, norm_w):
    B, S, _ = q_raw.shape
    f32 = jnp.float32
    q = jax.nn.silu(q_raw.astype(f32)).reshape(B, S, HG_HEADS, HG_KEY)
    lbh = lb.astype(f32).reshape(HG_HEADS, HG_KEY)
    f = lbh + (1.0 - lbh) * jax.nn.sigmoid(f_raw.astype(f32).reshape(B, S, HG_HEADS, HG_KEY))
    k = 1.0 - f
    logf = jnp.log(f)
    v = i_raw.astype(f32).reshape(B, S, HG_HEADS, HG_VAL)
    n_chunks = S // CHUNK

    def to_chunks(t):
        return t.reshape(B, n_chunks, CHUNK, HG_HEADS, t.shape[-1]).transpose(1, 0, 3, 2, 4)

    qc, kc, vc = to_chunks(q), to_chunks(k), to_chunks(v)
    bc = jnp.cumsum(to_chunks(logf), axis=3)
    causal = jnp.tril(jnp.ones((CHUNK, CHUNK), dtype=bool))

    def step(state, xs):
        qt, kt, vt, bt = xs
        inter = jnp.einsum('bhck,bhkv->bhcv', qt * jnp.exp(bt), state)
        rel = bt[:, :, :, None, :] - bt[:, :, None, :, :]
        decay = jnp.exp(jnp.where(causal[:, :, None], rel, -jnp.inf))
        scores = jnp.einsum('bhtk,bhsk,bhtsk->bhts', qt, kt, decay)
        intra = jnp.einsum('bhts,bhsv->bhtv', scores, vt)
        b_last = bt[:, :, -1, :]
        new_state = jnp.exp(b_last)[..., None] * state + jnp.einsum(
            'bhsk,bhsv->bhkv', kt * jnp.exp(b_last[:, :, None, :] - bt), vt)
        return new_state, inter + intra

    s0 = jnp.zeros((B, HG_HEADS, HG_KEY, HG_VAL), f32)
    _, oc = lax.scan(step, s0, (qc, kc, vc, bc))
    o = oc.transpose(1, 0, 3, 2, 4).reshape(B, S, HG_HEADS, HG_VAL)
    o = o * lax.rsqrt(jnp.mean(o * o, axis=-1, keepdims=True) + EPS)
    o = o.reshape(B, S, HG_WIDTH) * norm_w.astype(f32) * jax.nn.silu(g_raw.astype(f32))
    return o.astype(q_raw.dtype)


def dilated_branch(q, k, v, window, dilation):
    B, Sp, H, E = q.shape
    span = window // dilation
    nb = Sp // dilation // ATT_BLOCK

    def to_blocks(t):
        return t.reshape(B, nb, ATT_BLOCK, dilation, H, E).transpose(0, 3, 4, 1, 2, 5)

    def with_prev(t):
        prev = jnp.pad(t[:, :, :, :-1], ((0, 0), (0, 0), (0, 0), (1, 0), (0, 0), (0, 0)))
        return jnp.concatenate([prev, t], axis=4)

    qb = to_blocks(q)
    kw, vw = with_prev(to_blocks(k)), with_prev(to_blocks(v))
    s = jnp.einsum('brhnqe,brhnke->brhnqk', qb, kw) * (E ** -0.5)
    qi = jnp.arange(ATT_BLOCK)[:, None]
    kj = jnp.arange(2 * ATT_BLOCK)[None, :]
    dist = ATT_BLOCK + qi - kj
    blk = jnp.arange(nb)[:, None, None]
    valid = (dist >= 0) & (dist <= span) & ((blk > 0) | (kj >= ATT_BLOCK)[None])
    s = jnp.where(valid, s, -jnp.inf)
    lse = jax.nn.logsumexp(s, axis=-1)
    o = jnp.einsum('brhnqk,brhnke->brhnqe', jnp.exp(s - lse[..., None]), vw)
    o = o.transpose(0, 3, 4, 1, 2, 5).reshape(B, Sp, H, E)
    lse = lse.transpose(0, 3, 4, 1, 2).reshape(B, Sp, H)
    return o, lse


def dilated_attention_mixer(q_raw, k_raw, v_raw):
    B, S, _ = q_raw.shape
    unit = ATT_BLOCK * max(d for _, d in DILATED_BRANCHES)
    Sp = -(-S // unit) * unit

    def prep(t):
        t = t.astype(jnp.float32).reshape(B, S, ATT_HEADS, ATT_HEAD_DIM)
        return jnp.pad(t, ((0, 0), (0, Sp - S), (0, 0), (0, 0)))

    q, k, v = prep(q_raw), prep(k_raw), prep(v_raw)
    branches = [dilated_branch(q, k, v, w, d) for w, d in DILATED_BRANCHES]
    outs = jnp.stack([o for o, _ in branches])
    lses = jnp.stack([l for _, l in branches])
    wts = jax.nn.softmax(lses, axis=0)
    o = jnp.einsum('nbsh,nbshe->bshe', wts, outs)[:, :S]
    return o.reshape(B, S, ATT_WIDTH).astype(q_raw.dtype)


def conv_gated_mlp(u, w_up, conv_w, conv_b, w_down):
    S = u.shape[1]
    up = u @ w_up
    pad = jnp.pad(up, ((0, 0), (CONV_WIDTH - 1, 0), (0, 0)))
    up = conv_b + sum(pad[:, j:j + S] * conv_w[j] for j in range(CONV_WIDTH))
    gate, val = jnp.split(up, 2, axis=-1)
    return (jax.nn.gelu(gate, approximate=True) * val) @ w_down


def setup_inputs(seed: int = 0) -> dict:
    key = jax.random.key(seed)
    ks = jax.random.split(key, 20)
    f32 = jnp.float32

    def nrm(k, shape, scale):
        return jax.random.normal(k, shape, f32) * scale

    def gain(k, shape):
        return 1.0 + 0.05 * jax.random.normal(k, shape, f32)

    L = DEPTH
    return {
        'x': nrm(ks[0], (BATCH, SEQ, D_MODEL), 1.0),
        'p': nrm(ks[1], (DEPTH, BATCH, SEQ, PLE_DIM), 1.0),
        'ln_mix_pre': gain(ks[2], (L, D_MODEL)),
        'w_in': nrm(ks[3], (L, D_MODEL, IN_COLS), D_MODEL ** -0.5),
        'lb_logits': nrm(ks[4], (L, HG_WIDTH), 0.1),
        'hgrn_norm': gain(ks[5], (L, HG_WIDTH)),
        'w_out': nrm(ks[6], (L, MIX_WIDTH, D_MODEL), MIX_WIDTH ** -0.5),
        'ln_mix_post': gain(ks[7], (L, D_MODEL)),
        'ln_ffn_pre': gain(ks[8], (L, D_MODEL)),
        'w_up': nrm(ks[9], (L, D_MODEL, 2 * D_FF), D_MODEL ** -0.5),
        'conv_w': nrm(ks[10], (L, CONV_WIDTH, 2 * D_FF), CONV_WIDTH ** -0.5),
        'conv_b': nrm(ks[11], (L, 2 * D_FF), 0.02),
        'w_down': nrm(ks[12], (L, D_FF, D_MODEL), D_FF ** -0.5),
        'ln_ffn_post': gain(ks[13], (L, D_MODEL)),
        'w_pe': nrm(ks[14], (L, PLE_DIM, D_MODEL), PLE_DIM ** -0.5),
        'w_pg': nrm(ks[15], (L, D_MODEL, D_MODEL), D_MODEL ** -0.5),
    }


def reference(x, p, ln_mix_pre, w_in, lb_logits, hgrn_norm, w_out, ln_mix_post,
              ln_ffn_pre, w_up, conv_w, conv_b, w_down, ln_ffn_post, w_pe, w_pg):
    lb_all = jnp.cumsum(jax.nn.softmax(lb_logits.astype(jnp.float32), axis=0), axis=0)
    lb_all = lb_all - lb_all[0]
    split_at = [int(c) for c in np.cumsum(IN_SIZES)[:-1]]
    h = x
    for l in range(DEPTH):
        u = rms_norm(h, ln_mix_pre[l])
        hq, hf, hi, hg, aq, ak, av = jnp.split(u @ w_in[l], split_at, axis=-1)
        o_rec = hgrn2_mixer(hq, hf, hi, hg, lb_all[l], hgrn_norm[l])
        o_att = dilated_attention_mixer(aq, ak, av)
        mixed = jnp.concatenate([o_rec, o_att], axis=-1) @ w_out[l]
        h = h + rms_norm(mixed, ln_mix_post[l])
        y = conv_gated_mlp(rms_norm(h, ln_ffn_pre[l]), w_up[l], conv_w[l], conv_b[l], w_down[l])
        h = h + rms_norm(y, ln_ffn_post[l])
        h = h + (p[l] @ w_pe[l]) * jax.nn.sigmoid(h @ w_pg[l])
    return h
```

```python
import numpy as np
import ml_dtypes
import concourse.bass as bass
import concourse.mybir as mybir
from concourse.bass_utils import run_bass_kernel_spmd

F32 = mybir.dt.float32
BF16 = mybir.dt.bfloat16
AF = mybir.ActivationFunctionType
ALU = mybir.AluOpType
AX = mybir.AxisListType

D_MODEL = 2048
DEPTH = 4
SEQ = 16384
BATCH = 2
D_FF = 5632
PLE = 256
EPS = 1e-6
NG = 8
CS = D_MODEL // NG
MC = CS // 128
FG = D_FF // NG
TT = 512


class T:
    def __init__(self, k, ap, name):
        self.k, self.ap, self.name = k, ap, name
        self.last_w = None
        self.readers = {}
        self.dsem = None
        self.dcount = 0

    def __getitem__(self, idx):
        return self.ap[idx]


class K:
    def __init__(self):
        self.nc = bass.Bass("TRN2", target_bir_lowering=False)
        nc = self.nc
        self.h = {"pe": nc.tensor, "act": nc.scalar, "dve": nc.vector, "pool": nc.gpsimd, "sp": nc.sync}
        self.sem = {n: nc.alloc_semaphore("s_" + n) for n in self.h}
        self.cnt = {n: 0 for n in self.h}
        self.waited = {n: {} for n in self.h}
        self.ntile = 0
        self.out_dmas = []

    def sb(self, shape, dt, name=None):
        self.ntile += 1
        name = name or f"t{self.ntile}"
        return T(self, self.nc.alloc_sbuf_tensor(name, list(shape), dt).ap(), name)

    def ps(self, name=None, shape=(128, 512), dt=F32):
        self.ntile += 1
        name = name or f"p{self.ntile}"
        return T(self, self.nc.alloc_psum_tensor(name, list(shape), dt).ap(), name)

    def dram(self, name, shape, dt, kind):
        return self.nc.dram_tensor(name, list(shape), dt, kind=kind).ap()

    def _wait(self, e, deps):
        for key, (sem, val) in deps.items():
            if e == "pe" and key == "pe":
                continue
            if self.waited[e].get(key, 0) < val:
                self.h[e].wait_ge(sem, val)
                self.waited[e][key] = val

    def _deps(self, reads, writes):
        deps = {}

        def add(d):
            if d is None:
                return
            key, sem, val = d
            if key not in deps or deps[key][1] < val:
                deps[key] = (sem, val)

        for t in reads:
            add(t.last_w)
        for t in writes:
            add(t.last_w)
            for key, (sem, val) in t.readers.items():
                add((key, sem, val))
        return deps

    def op(self, e, fn, reads=(), writes=()):
        self._wait(e, self._deps(reads, writes))
        ins = fn()
        self.cnt[e] += 1
        ins.then_inc(self.sem[e], 1)
        val = self.cnt[e]
        for t in reads:
            t.readers[e] = (self.sem[e], val)
        for t in writes:
            t.last_w = (e, self.sem[e], val)
            t.readers = {}
        return ins

    def mm(self, out_t, out_ap, lhsT_t, lhsT_ap, rhs_t, rhs_ap, start, stop):
        e = "pe"
        self._wait(e, self._deps((lhsT_t, rhs_t), (out_t,)))
        ins = self.nc.tensor.matmul(out_ap, lhsT_ap, rhs_ap, start=start, stop=stop)
        if stop:
            self.cnt[e] += 1
            ins.then_inc(self.sem[e], 1)
            val = self.cnt[e]
        else:
            val = self.cnt[e] + 1
        for t in (lhsT_t, rhs_t):
            t.readers[e] = (self.sem[e], val)
        out_t.last_w = (e, self.sem[e], val)
        out_t.readers = {}
        return ins

    def _dsem(self, t):
        if t.dsem is None:
            t.dsem = self.nc.alloc_semaphore("d_" + t.name)
        return t.dsem

    def load(self, t, out_ap, in_ap, q="sp"):
        sem = self._dsem(t)
        self._wait(q, self._deps((), (t,)))
        self.h[q].dma_start(out=out_ap, in_=in_ap).then_inc(sem, 16)
        t.dcount += 16
        t.last_w = ("d_" + t.name, sem, t.dcount)
        t.readers = {}

    def store(self, t, out_ap, in_ap, q="pool", final=True):
        sem = self._dsem(t)
        self._wait(q, self._deps((t,), ()))
        self.h[q].dma_start(out=out_ap, in_=in_ap).then_inc(sem, 16)
        t.dcount += 16
        t.readers["d_" + t.name] = (sem, t.dcount)
        if final:
            self.out_dmas.append(t)

    def finish(self):
        seen = set()
        for t in self.out_dmas:
            if t.name in seen:
                continue
            seen.add(t.name)
            self.h["sp"].wait_ge(t.dsem, t.dcount)
        return self.nc


def consts(k):
    c = {}
    c["ones_bf"] = k.sb([128, 128], BF16, "ones_bf")
    k.op("dve", lambda: k.nc.vector.memset(c["ones_bf"].ap, 1.0), (), (c["ones_bf"],))
    c["ones_f"] = k.sb([128, 128], F32, "ones_f")
    k.op("dve", lambda: k.nc.vector.memset(c["ones_f"].ap, 1.0), (), (c["ones_f"],))
    c["eps"] = k.sb([128, 1], F32, "eps_c")
    k.op("dve", lambda: k.nc.vector.memset(c["eps"].ap, EPS), (), (c["eps"],))
    return c


def load_weight_bf16(k, w_dram, K_, N_, name, stage):
    KC = K_ // 128
    wb = k.sb([128, KC, N_], BF16, name)
    wv = w_dram.rearrange("(kc p) n -> p kc n", p=128)
    NS = stage[0].ap.shape[1]
    i = 0
    for kc in range(KC):
        for n0 in range(0, N_, NS):
            n1 = min(N_, n0 + NS)
            st = stage[i % len(stage)]
            k.load(st, st.ap[:, 0:n1 - n0], wv[:, kc, n0:n1])
            eng = "act" if i % 2 == 0 else "dve"
            if eng == "act":
                k.op("act", lambda: k.nc.scalar.copy(wb.ap[:, kc, n0:n1], st.ap[:, 0:n1 - n0]), (st,), (wb,))
            else:
                k.op("dve", lambda: k.nc.vector.tensor_copy(wb.ap[:, kc, n0:n1], st.ap[:, 0:n1 - n0]), (st,), (wb,))
            i += 1
    return wb


def rstd_from_parts(k, c, ssq4_dram, t0, tn, s4, ps_t, rs, dim):
    k.load(s4, s4.ap[:, 0:tn], ssq4_dram[:, t0:t0 + tn])
    k.mm(ps_t, ps_t.ap[:, 0:tn], c["ones_f"], c["ones_f"].ap[0:NG, :], s4, s4.ap[:, 0:tn], True, True)
    k.op("act", lambda: k.nc.scalar.activation(rs.ap[:, 0:tn], ps_t.ap[:, 0:tn], AF.Sqrt, bias=c["eps"].ap[:, 0:1],
                                               scale=1.0 / dim), (ps_t, c["eps"]), (rs,))
    k.op("dve", lambda: k.nc.vector.reciprocal(rs.ap[:, 0:tn], rs.ap[:, 0:tn]), (rs,), (rs,))


def ssq_out(k, c, sq_tiles, ps_t, row, ssq_dram, t0, tn):
    n = len(sq_tiles)
    for i, (tl, ap) in enumerate(sq_tiles):
        k.mm(ps_t, ps_t.ap[:, 0:tn], c["ones_f"], c["ones_f"].ap, tl, ap, i == 0, i == n - 1)
    k.op("act", lambda: k.nc.scalar.copy(row.ap[0:1, 0:tn], ps_t.ap[0:1, 0:tn]), (ps_t,), (row,))
    k.store(row, ssq_dram[0:1, t0:t0 + tn], row.ap[0:1, 0:tn])


def build_dense(S, K_):
    k = K()
    KC = K_ // 128
    xT = k.dram("xT", [K_, S], BF16, "ExternalInput")
    w = k.dram("w", [K_, CS], F32, "ExternalInput")
    yT = k.dram("yT", [CS, S], F32, "ExternalOutput")
    ssq = k.dram("ssq", [1, S], F32, "ExternalOutput")
    c = consts(k)
    stage = [k.sb([128, 512], F32, f"wst{i}") for i in range(3)]
    wb = load_weight_bf16(k, w, K_, CS, "wb", stage)
    xb = [k.sb([128, KC, TT], BF16, f"xb{i}") for i in range(2)]
    ysb = [k.sb([128, MC, TT], F32, f"ysb{i}") for i in range(2)]
    ysq = [k.sb([128, MC, TT], F32, f"ysq{i}") for i in range(2)]
    row = k.sb([1, TT], F32, "row")
    pb = [k.ps(f"pb{i}") for i in range(4)]
    pss = k.ps("pss")
    xv = xT.rearrange("(kc p) s -> p kc s", p=128)
    yv = yT.rearrange("(m p) s -> p m s", p=128)
    NT = S // TT

    def ld(t):
        for kc in range(KC):
            k.load(xb[t % 2], xb[t % 2].ap[:, kc, :], xv[:, kc, t * TT:(t + 1) * TT])

    ld(0)
    for t in range(NT):
        if t + 1 < NT:
            ld(t + 1)
        x = xb[t % 2]
        y, q = ysb[t % 2], ysq[t % 2]
        for m in range(MC):
            p = pb[m]
            for kc in range(KC):
                k.mm(p, p.ap, wb, wb.ap[:, kc, m * 128:(m + 1) * 128], x, x.ap[:, kc, :], kc == 0, kc == KC - 1)
            k.op("act", lambda: k.nc.scalar.copy(y.ap[:, m, :], p.ap), (p,), (y,))
            k.op("dve", lambda: k.nc.vector.tensor_tensor(q.ap[:, m, :], p.ap, y.ap[:, m, :], ALU.mult), (p, y), (q,))
        k.store(y, yv[:, :, t * TT:(t + 1) * TT], y.ap)
        ssq_out(k, c, [(q, q.ap[:, m, :]) for m in range(MC)], pss, row, ssq, t * TT, TT)
    return k.finish()


def build_prep(S):
    k = K()
    hT = k.dram("hT", [CS, S], F32, "ExternalInput")
    mT = k.dram("mT", [CS, S], F32, "ExternalInput")
    ssq4 = k.dram("ssq4", [NG, S], F32, "ExternalInput")
    wpost = k.dram("wpost", [128, MC], F32, "ExternalInput")
    wnext = k.dram("wnext", [128, MC], F32, "ExternalInput")
    hnT = k.dram("hnT", [CS, S], F32, "ExternalOutput")
    hbT = k.dram("hbT", [CS, S], BF16, "ExternalOutput")
    ssq = k.dram("ssq", [1, S], F32, "ExternalOutput")
    c = consts(k)
    wp = k.sb([128, MC], F32, "wp")
    wn = k.sb([128, MC], F32, "wn")
    k.load(wp, wp.ap, wpost)
    k.load(wn, wn.ap, wnext)
    hb_ = [k.sb([128, MC, TT], F32, f"h{i}") for i in range(2)]
    mb_ = [k.sb([128, MC, TT], F32, f"m{i}") for i in range(2)]
    ob_ = [k.sb([128, MC, TT], BF16, f"o{i}") for i in range(2)]
    sq_ = [k.sb([128, MC, TT], F32, f"q{i}") for i in range(2)]
    s4 = k.sb([NG, TT], F32, "s4")
    rs = k.sb([128, TT], F32, "rs")
    row = k.sb([1, TT], F32, "row")
    prs, pss = k.ps("prs"), k.ps("pss")
    hv = hT.rearrange("(m p) s -> p m s", p=128)
    mv = mT.rearrange("(m p) s -> p m s", p=128)
    hnv = hnT.rearrange("(m p) s -> p m s", p=128)
    hbv = hbT.rearrange("(m p) s -> p m s", p=128)
    NT = S // TT

    def ld(t):
        k.load(hb_[t % 2], hb_[t % 2].ap, hv[:, :, t * TT:(t + 1) * TT])
        k.load(mb_[t % 2], mb_[t % 2].ap, mv[:, :, t * TT:(t + 1) * TT])

    ld(0)
    for t in range(NT):
        if t + 1 < NT:
            ld(t + 1)
        h, m_, o, q = hb_[t % 2], mb_[t % 2], ob_[t % 2], sq_[t % 2]
        rstd_from_parts(k, c, ssq4, t * TT, TT, s4, prs, rs, float(D_MODEL))
        for m in range(MC):
            k.op("dve", lambda: k.nc.vector.tensor_tensor(m_.ap[:, m, :], m_.ap[:, m, :], rs.ap, ALU.mult), (m_, rs), (m_,))
            k.op("dve", lambda: k.nc.vector.scalar_tensor_tensor(h.ap[:, m, :], m_.ap[:, m, :], wp.ap[:, m:m + 1], h.ap[:, m, :],
                                                                 ALU.mult, ALU.add), (m_, wp, h), (h,))
            k.op("dve", lambda: k.nc.vector.tensor_scalar(o.ap[:, m, :], h.ap[:, m, :], wn.ap[:, m:m + 1], None, ALU.mult),
                 (h, wn), (o,))
            k.op("act", lambda: k.nc.scalar.activation(q.ap[:, m, :], h.ap[:, m, :], AF.Square), (h,), (q,))
        k.store(h, hnv[:, :, t * TT:(t + 1) * TT], h.ap)
        k.store(o, hbv[:, :, t * TT:(t + 1) * TT], o.ap)
        ssq_out(k, c, [(q, q.ap[:, m, :]) for m in range(MC)], pss, row, ssq, t * TT, TT)
    return k.finish()


_CACHE = {}


def _get(name, fn, *args):
    key = (name,) + args
    if key not in _CACHE:
        _CACHE[key] = fn(*args)
    return _CACHE[key]


def run(nc, in_maps):
    res = run_bass_kernel_spmd(nc, in_maps, core_ids=list(range(8)))
    return res.results


def build_ple(S):
    k = K()
    KC = D_MODEL // 128
    hbT = k.dram("hbT", [D_MODEL, S], BF16, "ExternalInput")
    hT = k.dram("hT", [CS, S], F32, "ExternalInput")
    pT = k.dram("pT", [PLE, S], F32, "ExternalInput")
    wpe = k.dram("wpe", [PLE, CS], F32, "ExternalInput")
    wpg = k.dram("wpg", [D_MODEL, CS], F32, "ExternalInput")
    wnext = k.dram("wnext", [128, MC], F32, "ExternalInput")
    hnT = k.dram("hnT", [CS, S], F32, "ExternalOutput")
    hwT = k.dram("hwT", [CS, S], BF16, "ExternalOutput")
    ssq = k.dram("ssq", [1, S], F32, "ExternalOutput")
    c = consts(k)
    stage = [k.sb([128, 512], F32, f"wst{i}") for i in range(3)]
    wg = load_weight_bf16(k, wpg, D_MODEL, CS, "wg", stage)
    we = load_weight_bf16(k, wpe, PLE, CS, "we", stage)
    wn = k.sb([128, MC], F32, "wn")
    k.load(wn, wn.ap, wnext)
    xb = [k.sb([128, KC, TT], BF16, f"xb{i}") for i in range(2)]
    pf = [k.sb([128, 2, TT], F32, f"pf{i}") for i in range(2)]
    pb16 = k.sb([128, 2, TT], BF16, "pb16")
    hb_ = [k.sb([128, MC, TT], F32, f"h{i}") for i in range(2)]
    ob_ = [k.sb([128, MC, TT], BF16, f"o{i}") for i in range(2)]
    sq_ = [k.sb([128, MC, TT], F32, f"q{i}") for i in range(2)]
    sg = [k.sb([128, TT], F32, f"sg{i}") for i in range(2)]
    row = k.sb([1, TT], F32, "row")
    pg = [k.ps(f"pg{i}") for i in range(2)]
    pe_ = [k.ps(f"pe{i}") for i in range(2)]
    pss = k.ps("pss")
    xv = hbT.rearrange("(kc p) s -> p kc s", p=128)
    pv = pT.rearrange("(kc p) s -> p kc s", p=128)
    hv = hT.rearrange("(m p) s -> p m s", p=128)
    hnv = hnT.rearrange("(m p) s -> p m s", p=128)
    hwv = hwT.rearrange("(m p) s -> p m s", p=128)
    NT = S // TT

    def ld(t):
        for kc in range(KC):
            k.load(xb[t % 2], xb[t % 2].ap[:, kc, :], xv[:, kc, t * TT:(t + 1) * TT])
        k.load(pf[t % 2], pf[t % 2].ap, pv[:, :, t * TT:(t + 1) * TT])
        k.load(hb_[t % 2], hb_[t % 2].ap, hv[:, :, t * TT:(t + 1) * TT])

    ld(0)
    for t in range(NT):
        if t + 1 < NT:
            ld(t + 1)
        x, pp, h, o, q = xb[t % 2], pf[t % 2], hb_[t % 2], ob_[t % 2], sq_[t % 2]
        k.op("dve", lambda: k.nc.vector.tensor_copy(pb16.ap, pp.ap), (pp,), (pb16,))
        for m in range(MC):
            g_, e_, s_ = pg[m % 2], pe_[m % 2], sg[m % 2]
            for kc in range(KC):
                k.mm(g_, g_.ap, wg, wg.ap[:, kc, m * 128:(m + 1) * 128], x, x.ap[:, kc, :], kc == 0, kc == KC - 1)
            for kc in range(2):
                k.mm(e_, e_.ap, we, we.ap[:, kc, m * 128:(m + 1) * 128], pb16, pb16.ap[:, kc, :], kc == 0, kc == 1)
            k.op("act", lambda: k.nc.scalar.activation(s_.ap, g_.ap, AF.Sigmoid), (g_,), (s_,))
            k.op("dve", lambda: k.nc.vector.tensor_tensor(s_.ap, e_.ap, s_.ap, ALU.mult), (e_, s_), (s_,))
            k.op("dve", lambda: k.nc.vector.tensor_tensor(h.ap[:, m, :], h.ap[:, m, :], s_.ap, ALU.add), (h, s_), (h,))
            k.op("dve", lambda: k.nc.vector.tensor_scalar(o.ap[:, m, :], h.ap[:, m, :], wn.ap[:, m:m + 1], None, ALU.mult),
                 (h, wn), (o,))
            k.op("act", lambda: k.nc.scalar.activation(q.ap[:, m, :], h.ap[:, m, :], AF.Square), (h,), (q,))
        k.store(h, hnv[:, :, t * TT:(t + 1) * TT], h.ap)
        k.store(o, hwv[:, :, t * TT:(t + 1) * TT], o.ap)
        ssq_out(k, c, [(q, q.ap[:, m, :]) for m in range(MC)], pss, row, ssq, t * TT, TT)
    return k.finish()


def build_ffn_up(S, NSEQ=2):
    k = K()
    KC = D_MODEL // 128
    NJ = (FG + 127) // 128
    rows = [min(128, FG - j * 128) for j in range(NJ)]
    xT = k.dram("xT", [D_MODEL, S], BF16, "ExternalInput")
    ssq4 = k.dram("ssq4", [NG, S], F32, "ExternalInput")
    w = k.dram("w", [D_MODEL, 2 * FG], F32, "ExternalInput")
    cw = k.dram("cw", [128, 2 * NJ, 3], F32, "ExternalInput")
    cb = k.dram("cb", [128, 2 * NJ], F32, "ExternalInput")
    gT = k.dram("gT", [FG, S], BF16, "ExternalOutput")
    c = consts(k)
    stage = [k.sb([128, 512], F32, f"wst{i}") for i in range(3)]
    wb = load_weight_bf16(k, w, D_MODEL, 2 * FG, "wb", stage)
    cwt = k.sb([128, 2 * NJ, 3], F32, "cwt")
    cbt = k.sb([128, 2 * NJ], F32, "cbt")
    k.load(cwt, cwt.ap, cw)
    k.load(cbt, cbt.ap, cb)
    halo = k.sb([128, 2 * NJ, 2], F32, "halo")
    xb = [k.sb([128, KC, TT], BF16, f"xb{i}") for i in range(2)]
    s4 = k.sb([NG, TT], F32, "s4")
    rs = k.sb([128, TT], F32, "rs")
    up = [k.sb([128, TT + 2], F32, f"up{i}") for i in range(2)]
    acc = [k.sb([128, TT], F32, f"acc{i}") for i in range(2)]
    t1 = k.sb([128, TT], F32, "t1")
    t2 = k.sb([128, TT], F32, "t2")
    gout = [k.sb([128, NJ, TT], BF16, f"gout{i}") for i in range(2)]
    pu = [k.ps(f"pu{i}") for i in range(4)]
    prs = k.ps("prs")
    xv = xT.rearrange("(kc p) s -> p kc s", p=128)
    NF = NJ - 1 if rows[-1] < 128 else NJ
    gv = gT[0:NF * 128, :].rearrange("(j p) s -> p j s", p=128)
    NT = S // TT
    NTS = NT // NSEQ

    def ld(t):
        for kc in range(KC):
            k.load(xb[t % 2], xb[t % 2].ap[:, kc, :], xv[:, kc, t * TT:(t + 1) * TT])

    ld(0)
    ip = 0
    for t in range(NT):
        if t + 1 < NT:
            ld(t + 1)
        if t % NTS == 0:
            k.op("dve", lambda: k.nc.vector.memset(halo.ap, 0.0), (), (halo,))
        x, go = xb[t % 2], gout[t % 2]
        rstd_from_parts(k, c, ssq4, t * TT, TT, s4, prs, rs, float(D_MODEL))
        for j in range(NJ):
            R = rows[j]
            for wh in range(2):
                ct = j + wh * NJ
                c0 = wh * FG + j * 128
                p = pu[ip % 4]
                ip += 1
                u, a = up[wh], acc[wh]
                for kc in range(KC):
                    k.mm(p, p.ap[0:R, :], wb, wb.ap[:, kc, c0:c0 + R], x, x.ap[:, kc, :], kc == 0, kc == KC - 1)
                k.op("dve", lambda: k.nc.vector.tensor_copy(u.ap[0:R, 0:2], halo.ap[0:R, ct, :]), (halo,), (u,))
                k.op("dve", lambda: k.nc.vector.tensor_tensor(u.ap[0:R, 2:TT + 2], p.ap[0:R, :], rs.ap[0:R, :], ALU.mult), (p, rs), (u,))
                k.op("dve", lambda: k.nc.vector.tensor_copy(halo.ap[0:R, ct, :], u.ap[0:R, TT:TT + 2]), (u,), (halo,))
                k.op("act", lambda: k.nc.scalar.activation(a.ap[0:R, :], u.ap[0:R, 2:TT + 2], AF.Identity, bias=cbt.ap[0:R, ct:ct + 1],
                                                           scale=cwt.ap[0:R, ct, 2:3]), (u, cbt, cwt), (a,))
                k.op("dve", lambda: k.nc.vector.scalar_tensor_tensor(a.ap[0:R, :], u.ap[0:R, 1:TT + 1], cwt.ap[0:R, ct, 1:2], a.ap[0:R, :],
                                                                     ALU.mult, ALU.add), (u, cwt, a), (a,))
                k.op("dve", lambda: k.nc.vector.scalar_tensor_tensor(a.ap[0:R, :], u.ap[0:R, 0:TT], cwt.ap[0:R, ct, 0:1], a.ap[0:R, :],
                                                                     ALU.mult, ALU.add), (u, cwt, a), (a,))
            ga, va = acc[0], acc[1]
            k.op("act", lambda: k.nc.scalar.activation(t1.ap[0:R, :], ga.ap[0:R, :], AF.Square), (ga,), (t1,))
            k.op("dve", lambda: k.nc.vector.tensor_scalar(t1.ap[0:R, :], t1.ap[0:R, :], 0.044715, 1.0, ALU.mult, ALU.add), (t1,), (t1,))
            k.op("dve", lambda: k.nc.vector.tensor_tensor(t1.ap[0:R, :], t1.ap[0:R, :], ga.ap[0:R, :], ALU.mult), (t1, ga), (t1,))
            k.op("act", lambda: k.nc.scalar.activation(t2.ap[0:R, :], t1.ap[0:R, :], AF.Sigmoid, scale=1.5957691216057308), (t1,), (t2,))
            k.op("dve", lambda: k.nc.vector.tensor_tensor(t2.ap[0:R, :], t2.ap[0:R, :], ga.ap[0:R, :], ALU.mult), (t2, ga), (t2,))
            k.op("dve", lambda: k.nc.vector.tensor_tensor(go.ap[0:R, j, :], t2.ap[0:R, :], va.ap[0:R, :], ALU.mult), (t2, va), (go,))
        k.store(go, gv[:, :, t * TT:(t + 1) * TT], go.ap[:, 0:NF, :])
        if NF < NJ:
            k.store(go, gT[NF * 128:FG, t * TT:(t + 1) * TT], go.ap[0:rows[-1], NF, :])
    return k.finish()


def full_barrier(k, tiles):
    engs = ["pe", "act", "dve", "pool", "sp"]
    for e in engs:
        for o in engs:
            if o != e and k.cnt[o] > 0 and k.waited[e].get(o, 0) < k.cnt[o]:
                k.h[e].wait_ge(k.sem[o], k.cnt[o])
                k.waited[e][o] = k.cnt[o]
        for t in tiles:
            if t.dsem is not None and t.dcount > 0 and k.waited[e].get("d_" + t.name, 0) < t.dcount:
                k.h[e].wait_ge(t.dsem, t.dcount)
                k.waited[e]["d_" + t.name] = t.dcount


def mixer_consts():
    s = np.arange(128)
    same = (s[:, None] // 64) == (s[None, :] // 64)
    le = s[:, None] <= s[None, :]
    mid = (s // 64) * 64 + 31
    D = same * (le.astype(np.float32) - (s[:, None] <= mid[None, :]).astype(np.float32))
    sel = np.zeros((128, 128), np.float32)
    sel[:, 0] = s <= 31
    sel[:, 1] = s < 64
    sel[:, 2] = (s >= 64) & (s <= 95)
    sel[:, 3] = s >= 64
    suf = (same & (s[:, None] > s[None, :])).astype(np.float32)
    cf = np.concatenate([D, -D, sel, suf], 1).astype(np.float32)
    tri = (same & le).astype(np.float32)
    mask2 = np.concatenate([(s[:, None] >= s[None, :]), (s[:, None] <= s[None, :])], 1).astype(np.float32)
    ident = np.eye(128, dtype=np.float32)
    cb = np.concatenate([tri, mask2, ident], 1).astype(ml_dtypes.bfloat16)
    return cf, cb


def build_mixer(S, dbg="ha", NSEQ=2):
    k = K()
    nc = k.nc
    KC = D_MODEL // 128
    xT = k.dram("xT", [D_MODEL, S], BF16, "ExternalInput")
    ssq4 = k.dram("ssq4", [NG, S], F32, "ExternalInput")
    wfm = k.dram("wfm", [D_MODEL, 768], F32, "ExternalInput")
    wtm = k.dram("wtm", [D_MODEL, 256], F32, "ExternalInput")
    lbl_fm = k.dram("lbl_fm", [128, 2, 4], F32, "ExternalInput")
    lbl_tm = k.dram("lbl_tm", [128, 4, 128], F32, "ExternalInput")
    lmask = k.dram("lmask", [128, 4], F32, "ExternalInput")
    hnorm = k.dram("hnorm", [128, 2], F32, "ExternalInput")
    cf_d = k.dram("cf", [128, 512], F32, "ExternalInput")
    cb_d = k.dram("cb", [128, 512], BF16, "ExternalInput")
    oT = k.dram("oT", [CS, S], BF16, "ExternalOutput")
    fmS = k.dram("fmS", [768, S], BF16, "Internal")
    logfS = k.dram("logfS", [S, 128], F32, "Internal")
    kftmS = k.dram("kftmS", [S, 128], BF16, "Internal")
    vtmS = k.dram("vtmS", [S, 128], BF16, "Internal")

    c = consts(k)
    cf = k.sb([128, 512], F32, "cf_sb")
    cb = k.sb([128, 512], BF16, "cb_sb")
    k.load(cf, cf.ap, cf_d)
    k.load(cb, cb.ap, cb_d)
    hn = k.sb([128, 2], F32, "hn")
    k.load(hn, hn.ap, hnorm)
    PB = [k.ps(f"pb{i}") for i in range(7)]
    PT = k.ps("ptr", shape=(128, 1024), dt=BF16)

    lm = k.sb([128, 4], F32, "lm")
    k.load(lm, lm.ap, lmask)
    lf_ = k.sb([128, 2, 4], F32, "lblf")
    lt_ = k.sb([128, 4, 128], F32, "lblt")
    k.load(lf_, lf_.ap, lbl_fm)
    k.load(lt_, lt_.ap, lbl_tm)
    lb_fm = k.sb([128, 2], F32, "lb_fm")
    oml_fm = k.sb([128, 2], F32, "oml_fm")
    lb_tm = k.sb([128, 128], F32, "lb_tm")
    oml_tm = k.sb([128, 128], F32, "oml_tm")
    tmpa = k.sb([128, 128], F32, "tmpa")
    tmpb = k.sb([128, 128], F32, "tmpb")
    V = nc.vector
    k.op("dve", lambda: V.tensor_tensor(tmpa.ap, lt_.ap[:, 0, :], lt_.ap[:, 1, :], ALU.max), (lt_,), (tmpa,))
    k.op("dve", lambda: V.tensor_tensor(tmpa.ap, tmpa.ap, lt_.ap[:, 2, :], ALU.max), (lt_, tmpa), (tmpa,))
    k.op("dve", lambda: V.tensor_tensor(tmpa.ap, tmpa.ap, lt_.ap[:, 3, :], ALU.max), (lt_, tmpa), (tmpa,))
    for j in range(4):
        k.op("dve", lambda: V.tensor_tensor(lt_.ap[:, j, :], lt_.ap[:, j, :], tmpa.ap, ALU.subtract), (lt_, tmpa), (lt_,))
    k.op("act", lambda: nc.scalar.activation(lt_.ap, lt_.ap, AF.Exp), (lt_,), (lt_,))
    k.op("dve", lambda: V.tensor_tensor(tmpa.ap, lt_.ap[:, 0, :], lt_.ap[:, 1, :], ALU.add), (lt_,), (tmpa,))
    k.op("dve", lambda: V.tensor_tensor(tmpa.ap, tmpa.ap, lt_.ap[:, 2, :], ALU.add), (lt_, tmpa), (tmpa,))
    k.op("dve", lambda: V.tensor_tensor(tmpa.ap, tmpa.ap, lt_.ap[:, 3, :], ALU.add), (lt_, tmpa), (tmpa,))
    k.op("dve", lambda: V.reciprocal(tmpa.ap, tmpa.ap), (tmpa,), (tmpa,))
    k.op("dve", lambda: V.tensor_scalar(tmpb.ap, lt_.ap[:, 0, :], lm.ap[:, 0:1], None, ALU.mult), (lt_, lm), (tmpb,))
    for j in range(1, 4):
        k.op("dve", lambda: V.scalar_tensor_tensor(tmpb.ap, lt_.ap[:, j, :], lm.ap[:, j:j + 1], tmpb.ap, ALU.mult, ALU.add),
             (lt_, lm, tmpb), (tmpb,))
    k.op("dve", lambda: V.tensor_tensor(lb_tm.ap, tmpb.ap, tmpa.ap, ALU.mult), (tmpa, tmpb), (lb_tm,))
    k.op("dve", lambda: V.tensor_scalar(oml_tm.ap, lb_tm.ap, -1.0, 1.0, ALU.mult, ALU.add), (lb_tm,), (oml_tm,))
    s2a = k.sb([128, 2], F32, "s2a")
    s2b = k.sb([128, 2], F32, "s2b")
    k.op("dve", lambda: V.tensor_tensor(s2a.ap, lf_.ap[:, :, 0], lf_.ap[:, :, 1], ALU.max), (lf_,), (s2a,))
    k.op("dve", lambda: V.tensor_tensor(s2a.ap, s2a.ap, lf_.ap[:, :, 2], ALU.max), (lf_, s2a), (s2a,))
    k.op("dve", lambda: V.tensor_tensor(s2a.ap, s2a.ap, lf_.ap[:, :, 3], ALU.max), (lf_, s2a), (s2a,))
    for j in range(4):
        k.op("dve", lambda: V.tensor_tensor(lf_.ap[:, :, j], lf_.ap[:, :, j], s2a.ap, ALU.subtract), (lf_, s2a), (lf_,))
    k.op("act", lambda: nc.scalar.activation(lf_.ap, lf_.ap, AF.Exp), (lf_,), (lf_,))
    k.op("dve", lambda: V.tensor_tensor(s2a.ap, lf_.ap[:, :, 0], lf_.ap[:, :, 1], ALU.add), (lf_,), (s2a,))
    k.op("dve", lambda: V.tensor_tensor(s2a.ap, s2a.ap, lf_.ap[:, :, 2], ALU.add), (lf_, s2a), (s2a,))
    k.op("dve", lambda: V.tensor_tensor(s2a.ap, s2a.ap, lf_.ap[:, :, 3], ALU.add), (lf_, s2a), (s2a,))
    k.op("dve", lambda: V.reciprocal(s2a.ap, s2a.ap), (s2a,), (s2a,))
    k.op("dve", lambda: V.tensor_scalar(s2b.ap, lf_.ap[:, :, 0], lm.ap[:, 0:1], None, ALU.mult), (lf_, lm), (s2b,))
    for j in range(1, 4):
        k.op("dve", lambda: V.scalar_tensor_tensor(s2b.ap, lf_.ap[:, :, j], lm.ap[:, j:j + 1], s2b.ap, ALU.mult, ALU.add),
             (lf_, lm, s2b), (s2b,))
    k.op("dve", lambda: V.tensor_tensor(lb_fm.ap, s2b.ap, s2a.ap, ALU.mult), (s2a, s2b), (lb_fm,))
    k.op("dve", lambda: V.tensor_scalar(oml_fm.ap, lb_fm.ap, -1.0, 1.0, ALU.mult, ALU.add), (lb_fm,), (oml_fm,))

    from contextlib import ExitStack
    p1_tiles = []
    with ExitStack() as es:
        def sbs(shape, dt, name):
            hnd = es.enter_context(nc.sbuf_tensor(name, list(shape), dt))
            t = T(k, hnd.ap(), name)
            p1_tiles.append(t)
            return t

        stage = [sbs([128, 512], F32, f"wst{i}") for i in range(3)]
        def ldw(w_dram, N_, name):
            wb = sbs([128, KC, N_], BF16, name)
            wv = w_dram.rearrange("(kc p) n -> p kc n", p=128)
            i = 0
            for kc in range(KC):
                for n0 in range(0, N_, 256):
                    st = stage[i % 3]
                    k.load(st, st.ap[:, 0:256], wv[:, kc, n0:n0 + 256])
                    if i % 2 == 0:
                        k.op("act", lambda: nc.scalar.copy(wb.ap[:, kc, n0:n0 + 256], st.ap[:, 0:256]), (st,), (wb,))
                    else:
                        k.op("dve", lambda: V.tensor_copy(wb.ap[:, kc, n0:n0 + 256], st.ap[:, 0:256]), (st,), (wb,))
                    i += 1
            return wb

        wf = ldw(wfm, 768, "wf")
        wt = ldw(wtm, 256, "wt")
        xb = [sbs([128, KC, TT], BF16, f"xb{i}") for i in range(2)]
        s4 = sbs([NG, TT], F32, "s4")
        rs = sbs([128, TT], F32, "rs")
        rt = sbs([128, 4], F32, "rt")
        fo = [sbs([128, 6, TT], BF16, f"fo{i}") for i in range(2)]
        u1 = [sbs([128, TT], F32, f"u1_{i}") for i in range(2)]
        lo = [sbs([128, 4, 128], F32, f"lo{i}") for i in range(2)]
        ko = [sbs([128, 4, 128], BF16, f"ko{i}") for i in range(2)]
        vo = [sbs([128, 4, 128], BF16, f"vo{i}") for i in range(2)]
        sg = [sbs([128, 128], F32, f"sg{i}") for i in range(2)]
        xv = xT.rearrange("(kc p) s -> p kc s", p=128)
        fmv = fmS.rearrange("(ct p) s -> p ct s", p=128)
        NT = S // TT

        def ld(t):
            for kc in range(KC):
                k.load(xb[t % 2], xb[t % 2].ap[:, kc, :], xv[:, kc, t * TT:(t + 1) * TT])

        ld(0)
        ip = 0
        for t in range(NT):
            if t + 1 < NT:
                ld(t + 1)
            x, f_o = xb[t % 2], fo[t % 2]
            rstd_from_parts(k, c, ssq4, t * TT, TT, s4, PB[6], rs, float(D_MODEL))
            for i in range(4):
                k.mm(PB[5], PB[5].ap[:, i * 128:(i + 1) * 128], s4, s4.ap[:, i * 128:(i + 1) * 128], c["ones_f"], c["ones_f"].ap[0:NG, :], True, True)
            k.op("act", lambda: nc.scalar.activation(rt.ap, PB[5].ap[:, 0:512:128], AF.Sqrt, bias=c["eps"].ap[:, 0:1], scale=1.0 / D_MODEL),
                 (PB[5], c["eps"]), (rt,))
            k.op("dve", lambda: V.reciprocal(rt.ap, rt.ap), (rt,), (rt,))
            for ct in (range(6) if "F" not in dbg else []):
                slot, hd = ct, 0
                p = PB[ip % 3]
                ip += 1
                for kc in range(KC):
                    k.mm(p, p.ap, wf, wf.ap[:, kc, ct * 128:(ct + 1) * 128], x, x.ap[:, kc, :], kc == 0, kc == KC - 1)
                if slot in (0, 2):
                    u = u1[ct % 2]
                    k.op("dve", lambda: V.tensor_tensor(u.ap, p.ap, rs.ap, ALU.mult), (p, rs), (u,))
                    k.op("act", lambda: nc.scalar.activation(f_o.ap[:, ct, :], u.ap, AF.Silu), (u,), (f_o,))
                elif slot == 1:
                    u = u1[ct % 2]
                    k.op("dve", lambda: V.tensor_tensor(u.ap, p.ap, rs.ap, ALU.mult), (p, rs), (u,))
                    k.op("act", lambda: nc.scalar.activation(u.ap, u.ap, AF.Sigmoid, scale=-1.0), (u,), (u,))
                    k.op("dve", lambda: V.tensor_scalar(f_o.ap[:, ct, :], u.ap, oml_fm.ap[:, hd:hd + 1], None, ALU.mult),
                         (u, oml_fm), (f_o,))
                else:
                    k.op("dve", lambda: V.tensor_tensor(f_o.ap[:, ct, :], p.ap, rs.ap, ALU.mult), (p, rs), (f_o,))
            for c4 in range(0, 6, 3):
                k.store(f_o, fmv[:, c4:c4 + 3, t * TT:(t + 1) * TT], f_o.ap[:, c4:c4 + 3, :])
            l_o, k_o, v_o = lo[t % 2], ko[t % 2], vo[t % 2]
            for i in (range(4) if "T" not in dbg else []):
                p = PB[3 + (i % 2)]
                s_ = sg[i % 2]
                for kc in range(KC):
                    k.mm(p, p.ap[:, 0:256], x, x.ap[:, kc, i * 128:(i + 1) * 128], wt, wt.ap[:, kc, :], kc == 0, kc == KC - 1)
                k.op("act", lambda: nc.scalar.activation(s_.ap, p.ap[:, 0:128], AF.Sigmoid, scale=rt.ap[:, i:i + 1]), (p, rt), (s_,))
                k.op("dve", lambda: V.tensor_tensor(s_.ap, s_.ap, oml_tm.ap, ALU.mult), (s_, oml_tm), (s_,))
                k.op("dve", lambda: V.tensor_tensor(s_.ap, s_.ap, lb_tm.ap, ALU.add), (s_, lb_tm), (s_,))
                k.op("act", lambda: nc.scalar.activation(l_o.ap[:, i, :], s_.ap, AF.Ln), (s_,), (l_o,))
                k.op("dve", lambda: V.tensor_scalar(k_o.ap[:, i, :], s_.ap, -1.0, 1.0, ALU.mult, ALU.add), (s_,), (k_o,))
                k.op("act", lambda: nc.scalar.activation(v_o.ap[:, i, :], p.ap[:, 128:256], AF.Identity, scale=rt.ap[:, i:i + 1]),
                     (p, rt), (v_o,))
            tv = lambda d_: d_[t * TT:(t + 1) * TT, :].rearrange("(i p) c -> p i c", p=128)
            k.store(l_o, tv(logfS), l_o.ap, final=True)
            k.store(k_o, tv(kftmS), k_o.ap, final=True)
            k.store(v_o, tv(vtmS), v_o.ap, final=True)
        full_barrier(k, p1_tiles)
    k.out_dmas = []

    if "h" not in dbg and "a" not in dbg:
        return k.finish()
    TB = 2048
    SSEQ = S // NSEQ
    NB = SSEQ // TB
    E_SCALE = 128 ** -0.5
    qf = k.sb([128, TB], BF16, "qf")
    kff = k.sb([128, TB], BF16, "kff")
    gf = k.sb([128, TB], BF16, "gf")
    aqf = k.sb([128, TB], BF16, "aqf")
    avf = k.sb([128, TB], BF16, "avf")
    akf = [[k.sb([128, TB], BF16, f"akf{h}_{s}") for s in range(2)] for h in range(1)]
    lgf = k.sb([128, 16, 128], F32, "lgf")
    kft = k.sb([128, 16, 128], BF16, "kft")
    vt = k.sb([128, 16, 128], BF16, "vt")
    Sst = [k.sb([128, 128], F32, f"Sst{h}") for h in range(1)]
    Sbf = [k.sb([128, 128], BF16, f"Sbf{i}") for i in range(2)]
    e12 = [k.sb([128, 256], F32, f"e12_{i}") for i in range(2)]
    e3 = [k.sb([128, 128], F32, f"e3_{i}") for i in range(2)]
    esl = [k.sb([128, 4], F32, f"esl{i}") for i in range(2)]
    qt_ = [k.sb([128, 128], BF16, f"qt{i}") for i in range(2)]
    kt_ = [k.sb([128, 128], BF16, f"kt{i}") for i in range(2)]
    kh_ = [k.sb([128, 2, 128], BF16, f"kh{i}") for i in range(2)]
    sc_ = [k.sb([128, 128], BF16, f"sc{i}") for i in range(2)]
    oraw = [k.sb([128, 512], F32, f"oraw{i}") for i in range(2)]
    osq = [k.sb([128, 512], BF16, f"osq{i}") for i in range(2)]
    rr = k.sb([128, 512], F32, "rr")
    orec = k.sb([128, TB], BF16, "orec")
    oatt = k.sb([128, TB], BF16, "oatt")
    vtu = k.sb([128, 48, 128], BF16, "vtu")
    vcar = [k.sb([128, 21, 128], BF16, f"vcar{h}") for h in range(1)]
    pT = [k.sb([128, 256], BF16, f"pT{i}") for i in range(2)]
    UZ = k.sb([128, 2, TB], F32, "UZ")
    A = nc.scalar
    tri = cb.ap[:, 0:128]
    mask2 = cb.ap[:, 128:384]
    ident = cb.ap[:, 384:512]
    pA, pB_, pC, pD, pE, pF, pG = PB

    for sq in range(NSEQ):
        for B in range(NB):
            hd = 0
            t0 = sq * SSEQ + B * TB
            if B == 0:
                k.op("dve", lambda: V.memset(Sst[0].ap, 0.0), (), (Sst[0],))
            cur, prv = akf[hd][B % 2], akf[hd][(B + 1) % 2]
            for (tl, slot) in ((qf, 0), (kff, 1), (gf, 2), (aqf, 3), (cur, 4), (avf, 5)):
                r0 = slot * 128
                k.load(tl, tl.ap, fmS[r0:r0 + 128, t0:t0 + TB])
            tmv = lambda d_: d_[t0:t0 + TB, :].rearrange("(i p) c -> p i c", p=128)
            for i4 in range(0, 16, 4):
                k.load(lgf, lgf.ap[:, i4:i4 + 4, :], tmv(logfS)[:, i4:i4 + 4, :])
                k.load(kft, kft.ap[:, i4:i4 + 4, :], tmv(kftmS)[:, i4:i4 + 4, :])
                k.load(vt, vt.ap[:, i4:i4 + 4, :], tmv(vtmS)[:, i4:i4 + 4, :])
            S_ = Sst[hd]
            for i in (range(16) if "h" in dbg else []):
                a = i % 2
                lf = lgf.ap[:, i, :]
                tsl = slice(i * 128, (i + 1) * 128)
                k.mm(pA, pA.ap[:, 0:256], lgf, lf, cf, cf.ap[:, 0:256], True, True)
                k.mm(pA, pA.ap[:, 256:384], lgf, lf, cf, cf.ap[:, 256:384], True, True)
                k.mm(pB_, pB_.ap[:, 0:128], cf, cf.ap[:, 384:512], lgf, lf, True, True)
                k.op("act", lambda: A.activation(e12[a].ap, pA.ap[:, 0:256], AF.Exp), (pA,), (e12[a],))
                k.op("act", lambda: A.activation(esl[a].ap, pA.ap[:, 256:260], AF.Exp), (pA,), (esl[a],))
                k.op("act", lambda: A.activation(e3[a].ap, pB_.ap[:, 0:128], AF.Exp), (pB_,), (e3[a],))
                k.op("dve", lambda: V.tensor_tensor(qt_[a].ap, qf.ap[:, tsl], e12[a].ap[:, 0:128], ALU.mult), (qf, e12[a]), (qt_[a],))
                k.op("dve", lambda: V.tensor_tensor(kt_[a].ap, kff.ap[:, tsl], e12[a].ap[:, 128:256], ALU.mult), (kff, e12[a]), (kt_[a],))
                for ch in range(2):
                    k.op("dve", lambda: V.scalar_tensor_tensor(kh_[a].ap[:, ch, :], kft.ap[:, i, :], cf.ap[:, 257 + 2 * ch:258 + 2 * ch], e3[a].ap,
                                                               ALU.mult, ALU.mult), (kft, cf, e3[a]), (kh_[a],))
                if "1" in dbg:
                    continue
                k.mm(pC, pC.ap[:, 0:128], kt_[a], kt_[a].ap, qt_[a], qt_[a].ap, True, True)
                k.op("dve", lambda: V.tensor_tensor(sc_[a].ap, pC.ap[:, 0:128], tri, ALU.mult), (pC, cb), (sc_[a],))
                if "2" in dbg:
                    continue
                for ch in range(2):
                    k.mm(pD, pD.ap[:, ch * 128:(ch + 1) * 128], kh_[a], kh_[a].ap[:, ch, :], vt, vt.ap[:, i, :], True, True)
                k.op("dve", lambda: V.tensor_scalar(Sbf[0].ap, S_.ap, esl[a].ap[:, 0:1], None, ALU.mult), (S_, esl[a]), (Sbf[0],))
                k.op("dve", lambda: V.scalar_tensor_tensor(S_.ap, S_.ap, esl[a].ap[:, 1:2], pD.ap[:, 0:128], ALU.mult, ALU.add),
                     (S_, esl[a], pD), (S_,))
                k.op("dve", lambda: V.tensor_scalar(Sbf[1].ap, S_.ap, esl[a].ap[:, 2:3], None, ALU.mult), (S_, esl[a]), (Sbf[1],))
                k.op("dve", lambda: V.scalar_tensor_tensor(S_.ap, S_.ap, esl[a].ap[:, 3:4], pD.ap[:, 128:256], ALU.mult, ALU.add),
                     (S_, esl[a], pD), (S_,))
                if "3" in dbg:
                    continue
                k.mm(pE, pE.ap[:, 0:128], vt, vt.ap[:, i, :], sc_[a], sc_[a].ap, True, True)
                for ch in range(2):
                    k.mm(pE, pE.ap[:, 128 + ch * 64:128 + (ch + 1) * 64], Sbf[ch], Sbf[ch].ap, qt_[a], qt_[a].ap[:, ch * 64:(ch + 1) * 64],
                         True, True)
                ob = (i // 4) % 2
                osl = slice((i % 4) * 128, (i % 4 + 1) * 128)
                k.op("act", lambda: A.copy(oraw[ob].ap[:, osl], pE.ap[:, 0:128]), (pE,), (oraw[ob],))
                k.op("dve", lambda: V.tensor_tensor(oraw[ob].ap[:, osl], oraw[ob].ap[:, osl], pE.ap[:, 128:256], ALU.add),
                     (oraw[ob], pE), (oraw[ob],))
                k.op("act", lambda: A.activation(osq[ob].ap[:, osl], oraw[ob].ap[:, osl], AF.Square), (oraw[ob],), (osq[ob],))
                if i % 4 == 3:
                    g0 = (i // 4) * 512
                    k.mm(pF, pF.ap, c["ones_bf"], c["ones_bf"].ap, osq[ob], osq[ob].ap, True, True)
                    k.op("act", lambda: A.activation(rr.ap, pF.ap, AF.Sqrt, bias=c["eps"].ap[:, 0:1], scale=1.0 / 128), (pF, c["eps"]), (rr,))
                    k.op("dve", lambda: V.reciprocal(rr.ap, rr.ap), (rr,), (rr,))
                    k.op("dve", lambda: V.tensor_tensor(rr.ap, rr.ap, oraw[ob].ap, ALU.mult), (rr, oraw[ob]), (rr,))
                    k.op("dve", lambda: V.scalar_tensor_tensor(orec.ap[:, g0:g0 + 512], rr.ap, hn.ap[:, hd:hd + 1], gf.ap[:, g0:g0 + 512],
                                                               ALU.mult, ALU.mult), (rr, hn, gf), (orec,))
            k.store(orec, oT[0:128, t0:t0 + TB], orec.ap)
            if "a" not in dbg:
                continue
            for bi, d in enumerate((1, 4, 16)):
                R, U = d, 16 // d
                for u0 in range(0, 16, 4):
                    for uu in range(4):
                        u = u0 + uu
                        n_, r_ = u // R, u % R
                        st_ = n_ * 128 * d + r_
                        k.op("pe", lambda: nc.tensor.transpose(PT.ap[:, uu * 128:(uu + 1) * 128], avf.ap[:, st_:st_ + 127 * d + 1:d], ident),
                             (avf, cb), (PT,))
                    k.op("act", lambda: A.copy(vtu.ap[:, bi * 16 + u0:bi * 16 + u0 + 4, :],
                                               PT.ap[:, 0:512].rearrange("p (u e) -> p u e", u=4)), (PT,), (vtu,))
            iu = 0
            for bi, d in enumerate((1, 4, 16)):
                R, U = d, 16 // d
                coff = (0, 1, 5)[bi]
                for u in range(16):
                    n_, r_ = u // R, u % R
                    st_ = n_ * 128 * d + r_
                    qsl = slice(st_, st_ + 127 * d + 1, d)
                    has_prev = (n_ > 0) or (B > 0)
                    p_ = pT[iu % 2]
                    sps = pA if iu % 2 == 0 else pB_
                    ups = pC if iu % 2 == 0 else pD
                    iu += 1
                    k.mm(sps, sps.ap[:, 128:256], cur, cur.ap[:, qsl], aqf, aqf.ap[:, qsl], True, True)
                    if has_prev:
                        if n_ > 0:
                            pst = (n_ - 1) * 128 * d + r_
                            ksrc, vsrc_t, vsrc = cur, vtu, vtu.ap[:, bi * 16 + u - R, :]
                        else:
                            pst = (U - 1) * 128 * d + r_
                            ksrc, vsrc_t, vsrc = prv, vcar[hd], vcar[hd].ap[:, coff + r_, :]
                        k.mm(sps, sps.ap[:, 0:128], ksrc, ksrc.ap[:, pst:pst + 127 * d + 1:d], aqf, aqf.ap[:, qsl], True, True)
                        k.op("act", lambda: A.activation(p_.ap, sps.ap[:, 0:256], AF.Exp, scale=E_SCALE), (sps,), (p_,))
                        k.op("dve", lambda: V.tensor_tensor(p_.ap, p_.ap, mask2, ALU.mult), (p_, cb), (p_,))
                        k.mm(ups, ups.ap[:, 0:128], vsrc_t, vsrc, p_, p_.ap[:, 0:128], True, False)
                        k.mm(ups, ups.ap[:, 0:128], vtu, vtu.ap[:, bi * 16 + u, :], p_, p_.ap[:, 128:256], False, True)
                        k.mm(ups, ups.ap[:, 128:256], c["ones_bf"], c["ones_bf"].ap, p_, p_.ap[:, 0:128], True, False)
                        k.mm(ups, ups.ap[:, 128:256], c["ones_bf"], c["ones_bf"].ap, p_, p_.ap[:, 128:256], False, True)
                    else:
                        k.op("act", lambda: A.activation(p_.ap[:, 128:256], sps.ap[:, 128:256], AF.Exp, scale=E_SCALE), (sps,), (p_,))
                        k.op("dve", lambda: V.tensor_tensor(p_.ap[:, 128:256], p_.ap[:, 128:256], cb.ap[:, 256:384], ALU.mult), (p_, cb), (p_,))
                        k.mm(ups, ups.ap[:, 0:128], vtu, vtu.ap[:, bi * 16 + u, :], p_, p_.ap[:, 128:256], True, True)
                        k.mm(ups, ups.ap[:, 128:256], c["ones_bf"], c["ones_bf"].ap, p_, p_.ap[:, 128:256], True, True)
                    src = ups.ap[:, 0:256].rearrange("p (a q) -> p a q", a=2)
                    if bi == 0:
                        k.op("act", lambda: A.copy(UZ.ap[:, :, qsl], src), (ups,), (UZ,))
                    else:
                        k.op("dve", lambda: V.tensor_tensor(UZ.ap[:, :, qsl], UZ.ap[:, :, qsl], src, ALU.add), (UZ, ups), (UZ,))
            k.op("dve", lambda: V.tensor_copy(vcar[hd].ap[:, 0:1, :], vtu.ap[:, 15:16, :]), (vtu,), (vcar[hd],))
            k.op("dve", lambda: V.tensor_copy(vcar[hd].ap[:, 1:5, :], vtu.ap[:, 28:32, :]), (vtu,), (vcar[hd],))
            k.op("dve", lambda: V.tensor_copy(vcar[hd].ap[:, 5:21, :], vtu.ap[:, 32:48, :]), (vtu,), (vcar[hd],))
            k.op("dve", lambda: V.reciprocal(UZ.ap[:, 1, :], UZ.ap[:, 1, :]), (UZ,), (UZ,))
            k.op("dve", lambda: V.tensor_tensor(oatt.ap, UZ.ap[:, 0, :], UZ.ap[:, 1, :], ALU.mult), (UZ,), (oatt,))
            k.store(oatt, oT[128:256, t0:t0 + TB], oatt.ap)
    return k.finish()


def _vm(v):
    return np.ascontiguousarray(np.asarray(v, np.float32).reshape(MC, 128).T)


def kernel(x, p, ln_mix_pre, w_in, lb_logits, hgrn_norm, w_out, ln_mix_post, ln_ffn_pre, w_up, conv_w, conv_b,
           w_down, ln_ffn_post, w_pe, w_pg):
    x = np.asarray(x, np.float32)
    Bn, S, D = x.shape
    L = np.asarray(w_in).shape[0]
    assert D == D_MODEL
    S2 = Bn * S
    prep = _get("prep", build_prep, S2)
    mixer = _get("mixer", build_mixer, S2, "ha", Bn)
    d2k = _get("dense", build_dense, S2, D_MODEL)
    d5k = _get("dense", build_dense, S2, D_FF)
    upp = _get("up", build_ffn_up, S2, Bn)
    plep = _get("ple", build_ple, S2)
    cf, cb = mixer_consts()
    onesm = np.ones((128, MC), np.float32)
    cat = lambda lst: np.ascontiguousarray(np.concatenate(lst, 0))
    G = range(NG)
    cs = lambda g: slice(CS * g, CS * (g + 1))

    hsh = [np.ascontiguousarray(np.concatenate([x[b].T[cs(g)] for b in range(Bn)], 1)) for g in G]
    zeros = np.zeros((CS, S2), np.float32)
    onesq = np.ones((NG, S2), np.float32)
    res = run(prep, [{"hT": hsh[g], "mT": zeros, "ssq4": onesq, "wpost": onesm,
                      "wnext": _vm(np.asarray(ln_mix_pre[0])[cs(g)])} for g in G])
    del zeros
    hw = cat([r["hbT"] for r in res])
    ssq = cat([r["ssq"] for r in res])
    lbl = np.asarray(lb_logits, np.float32)
    NJ = (FG + 127) // 128
    for l in range(L):
        win = np.asarray(w_in[l], np.float32)
        lmask = np.zeros((128, 4), np.float32)
        lmask[:, 1:l + 1] = 1.0
        ins = []
        for g in G:
            sl = lambda slot: win[:, slot * 1024 + 128 * g: slot * 1024 + 128 * (g + 1)]
            wfm = np.ascontiguousarray(np.concatenate([sl(0), sl(1), sl(3), sl(4), sl(5), sl(6)], 1))
            wtm = np.ascontiguousarray(np.concatenate([sl(1), sl(2)], 1))
            lg = lbl[:, 128 * g:128 * (g + 1)]
            lbl_fm = np.ascontiguousarray(np.broadcast_to(lg.T[:, None, :], (128, 2, 4)))
            lbl_tm = np.ascontiguousarray(np.broadcast_to(lg[None], (128, 4, 128)))
            hv = np.asarray(hgrn_norm[l], np.float32)[128 * g:128 * (g + 1)]
            hnm = np.ascontiguousarray(np.stack([hv, hv], 1))
            ins.append({"xT": hw, "ssq4": ssq, "wfm": wfm, "wtm": wtm, "lbl_fm": lbl_fm, "lbl_tm": lbl_tm, "lmask": lmask,
                        "hnorm": hnm, "cf": cf, "cb": cb})
        res = run(mixer, ins)
        del ins, hw
        ofull = cat([r["oT"][0:128] for r in res] + [r["oT"][128:256] for r in res])
        del res
        res = run(d2k, [{"xT": ofull, "w": np.ascontiguousarray(np.asarray(w_out[l], np.float32)[:, cs(g)])} for g in G])
        del ofull
        ssq = cat([r["ssq"] for r in res])
        res = run(prep, [{"hT": hsh[g], "mT": res[g]["yT"], "ssq4": ssq, "wpost": _vm(np.asarray(ln_mix_post[l])[cs(g)]),
                          "wnext": _vm(np.asarray(ln_ffn_pre[l])[cs(g)])} for g in G])
        hsh = [r["hnT"] for r in res]
        hw = cat([r["hbT"] for r in res])
        ssq = cat([r["ssq"] for r in res])
        del res
        wup = np.asarray(w_up[l], np.float32)
        cwl = np.asarray(conv_w[l], np.float32)
        cbl = np.asarray(conv_b[l], np.float32)
        ins = []
        for g in G:
            cols = np.concatenate([np.arange(FG * g, FG * (g + 1)), D_FF + np.arange(FG * g, FG * (g + 1))])
            cwp = np.zeros((128, 2 * NJ, 3), np.float32)
            cbp = np.zeros((128, 2 * NJ), np.float32)
            for wh in range(2):
                cc = cols[wh * FG:(wh + 1) * FG]
                for j in range(NJ):
                    cj = cc[j * 128:(j + 1) * 128]
                    cwp[:len(cj), wh * NJ + j, :] = cwl[:, cj].T
                    cbp[:len(cj), wh * NJ + j] = cbl[cj]
            ins.append({"xT": hw, "ssq4": ssq, "w": np.ascontiguousarray(wup[:, cols]), "cw": cwp, "cb": cbp})
        res = run(upp, ins)
        del ins, hw
        gfull = cat([r["gT"] for r in res])
        del res
        res = run(d5k, [{"xT": gfull, "w": np.ascontiguousarray(np.asarray(w_down[l], np.float32)[:, cs(g)])} for g in G])
        del gfull
        ssq = cat([r["ssq"] for r in res])
        res = run(prep, [{"hT": hsh[g], "mT": res[g]["yT"], "ssq4": ssq, "wpost": _vm(np.asarray(ln_ffn_post[l])[cs(g)]),
                          "wnext": onesm} for g in G])
        hsh = [r["hnT"] for r in res]
        hb = cat([r["hbT"] for r in res])
        del res
        nxt = (lambda g: _vm(np.asarray(ln_mix_pre[l + 1])[cs(g)])) if l + 1 < L else (lambda g: onesm)
        pT = np.ascontiguousarray(np.concatenate([np.asarray(p[l][b], np.float32).T for b in range(Bn)], 1))
        res = run(plep, [{"hbT": hb, "hT": hsh[g], "pT": pT,
                          "wpe": np.ascontiguousarray(np.asarray(w_pe[l], np.float32)[:, cs(g)]),
                          "wpg": np.ascontiguousarray(np.asarray(w_pg[l], np.float32)[:, cs(g)]),
                          "wnext": nxt(g)} for g in G])
        del hb, pT
        hsh = [r["hnT"] for r in res]
        hw = cat([r["hwT"] for r in res])
        ssq = cat([r["ssq"] for r in res])
        del res
    out = np.empty((Bn, S, D), np.float32)
    for g in G:
        for b in range(Bn):
            out[b][:, cs(g)] = hsh[g][:, b * S:(b + 1) * S].T
    return out
```

```python
import numpy as np
import ml_dtypes
import concourse.bass as bass
import concourse.mybir as mybir
from concourse.bass_utils import run_bass_kernel_spmd

F32 = mybir.dt.float32
BF16 = mybir.dt.bfloat16
AF = mybir.ActivationFunctionType
ALU = mybir.AluOpType
AX = mybir.AxisListType

D_MODEL = 2048
DEPTH = 4
SEQ = 16384
BATCH = 2
D_FF = 5632
PLE = 256
EPS = 1e-6
NG = 8
CS = D_MODEL // NG
MC = CS // 128
FG = D_FF // NG
TT = 512


class T:
    def __init__(self, k, ap, name):
        self.k, self.ap, self.name = k, ap, name
        self.last_w = None
        self.readers = {}
        self.dsem = None
        self.dcount = 0

    def __getitem__(self, idx):
        return self.ap[idx]


class K:
    def __init__(self):
        self.nc = bass.Bass("TRN2", target_bir_lowering=False)
        nc = self.nc
        self.h = {"pe": nc.tensor, "act": nc.scalar, "dve": nc.vector, "pool": nc.gpsimd, "sp": nc.sync}
        self.sem = {n: nc.alloc_semaphore("s_" + n) for n in self.h}
        self.cnt = {n: 0 for n in self.h}
        self.waited = {n: {} for n in self.h}
        self.ntile = 0
        self.out_dmas = []

    def sb(self, shape, dt, name=None):
        self.ntile += 1
        name = name or f"t{self.ntile}"
        return T(self, self.nc.alloc_sbuf_tensor(name, list(shape), dt).ap(), name)

    def ps(self, name=None, shape=(128, 512), dt=F32):
        self.ntile += 1
        name = name or f"p{self.ntile}"
        return T(self, self.nc.alloc_psum_tensor(name, list(shape), dt).ap(), name)

    def dram(self, name, shape, dt, kind):
        return self.nc.dram_tensor(name, list(shape), dt, kind=kind).ap()

    def _wait(self, e, deps):
        for key, (sem, val) in deps.items():
            if e == "pe" and key == "pe":
                continue
            if self.waited[e].get(key, 0) < val:
                self.h[e].wait_ge(sem, val)
                self.waited[e][key] = val

    def _deps(self, reads, writes):
        deps = {}

        def add(d):
            if d is None:
                return
            key, sem, val = d
            if key not in deps or deps[key][1] < val:
                deps[key] = (sem, val)

        for t in reads:
            add(t.last_w)
        for t in writes:
            add(t.last_w)
            for key, (sem, val) in t.readers.items():
                add((key, sem, val))
        return deps

    def op(self, e, fn, reads=(), writes=()):
        self._wait(e, self._deps(reads, writes))
        ins = fn()
        self.cnt[e] += 1
        ins.then_inc(self.sem[e], 1)
        val = self.cnt[e]
        for t in reads:
            t.readers[e] = (self.sem[e], val)
        for t in writes:
            t.last_w = (e, self.sem[e], val)
            t.readers = {}
        return ins

    def mm(self, out_t, out_ap, lhsT_t, lhsT_ap, rhs_t, rhs_ap, start, stop):
        e = "pe"
        self._wait(e, self._deps((lhsT_t, rhs_t), (out_t,)))
        ins = self.nc.tensor.matmul(out_ap, lhsT_ap, rhs_ap, start=start, stop=stop)
        if stop:
            self.cnt[e] += 1
            ins.then_inc(self.sem[e], 1)
            val = self.cnt[e]
        else:
            val = self.cnt[e] + 1
        for t in (lhsT_t, rhs_t):
            t.readers[e] = (self.sem[e], val)
        out_t.last_w = (e, self.sem[e], val)
        out_t.readers = {}
        return ins

    def _dsem(self, t):
        if t.dsem is None:
            t.dsem = self.nc.alloc_semaphore("d_" + t.name)
        return t.dsem

    def load(self, t, out_ap, in_ap, q="sp"):
        sem = self._dsem(t)
        self._wait(q, self._deps((), (t,)))
        self.h[q].dma_start(out=out_ap, in_=in_ap).then_inc(sem, 16)
        t.dcount += 16
        t.last_w = ("d_" + t.name, sem, t.dcount)
        t.readers = {}

    def store(self, t, out_ap, in_ap, q="pool", final=True):
        sem = self._dsem(t)
        self._wait(q, self._deps((t,), ()))
        self.h[q].dma_start(out=out_ap, in_=in_ap).then_inc(sem, 16)
        t.dcount += 16
        t.readers["d_" + t.name] = (sem, t.dcount)
        if final:
            self.out_dmas.append(t)

    def finish(self):
        seen = set()
        for t in self.out_dmas:
            if t.name in seen:
                continue
            seen.add(t.name)
            self.h["sp"].wait_ge(t.dsem, t.dcount)
        return self.nc


def consts(k):
    c = {}
    c["ones_bf"] = k.sb([128, 128], BF16, "ones_bf")
    k.op("dve", lambda: k.nc.vector.memset(c["ones_bf"].ap, 1.0), (), (c["ones_bf"],))
    c["ones_f"] = k.sb([128, 128], F32, "ones_f")
    k.op("dve", lambda: k.nc.vector.memset(c["ones_f"].ap, 1.0), (), (c["ones_f"],))
    c["eps"] = k.sb([128, 1], F32, "eps_c")
    k.op("dve", lambda: k.nc.vector.memset(c["eps"].ap, EPS), (), (c["eps"],))
    return c


def load_weight_bf16(k, w_dram, K_, N_, name, stage):
    KC = K_ // 128
    wb = k.sb([128, KC, N_], BF16, name)
    wv = w_dram.rearrange("(kc p) n -> p kc n", p=128)
    NS = stage[0].ap.shape[1]
    i = 0
    for kc in range(KC):
        for n0 in range(0, N_, NS):
            n1 = min(N_, n0 + NS)
            st = stage[i % len(stage)]
            k.load(st, st.ap[:, 0:n1 - n0], wv[:, kc, n0:n1])
            eng = "act" if i % 2 == 0 else "dve"
            if eng == "act":
                k.op("act", lambda: k.nc.scalar.copy(wb.ap[:, kc, n0:n1], st.ap[:, 0:n1 - n0]), (st,), (wb,))
            else:
                k.op("dve", lambda: k.nc.vector.tensor_copy(wb.ap[:, kc, n0:n1], st.ap[:, 0:n1 - n0]), (st,), (wb,))
            i += 1
    return wb


def rstd_from_parts(k, c, ssq4_dram, t0, tn, s4, ps_t, rs, dim):
    k.load(s4, s4.ap[:, 0:tn], ssq4_dram[:, t0:t0 + tn])
    k.mm(ps_t, ps_t.ap[:, 0:tn], c["ones_f"], c["ones_f"].ap[0:NG, :], s4, s4.ap[:, 0:tn], True, True)
    k.op("act", lambda: k.nc.scalar.activation(rs.ap[:, 0:tn], ps_t.ap[:, 0:tn], AF.Sqrt, bias=c["eps"].ap[:, 0:1],
                                               scale=1.0 / dim), (ps_t, c["eps"]), (rs,))
    k.op("dve", lambda: k.nc.vector.reciprocal(rs.ap[:, 0:tn], rs.ap[:, 0:tn]), (rs,), (rs,))


def ssq_out(k, c, sq_tiles, ps_t, row, ssq_dram, t0, tn):
    n = len(sq_tiles)
    for i, (tl, ap) in enumerate(sq_tiles):
        k.mm(ps_t, ps_t.ap[:, 0:tn], c["ones_f"], c["ones_f"].ap, tl, ap, i == 0, i == n - 1)
    k.op("act", lambda: k.nc.scalar.copy(row.ap[0:1, 0:tn], ps_t.ap[0:1, 0:tn]), (ps_t,), (row,))
    k.store(row, ssq_dram[0:1, t0:t0 + tn], row.ap[0:1, 0:tn])


def build_dense(S, K_):
    k = K()
    KC = K_ // 128
    xT = k.dram("xT", [K_, S], BF16, "ExternalInput")
    w = k.dram("w", [K_, CS], F32, "ExternalInput")
    yT = k.dram("yT", [CS, S], F32, "ExternalOutput")
    ssq = k.dram("ssq", [1, S], F32, "ExternalOutput")
    c = consts(k)
    stage = [k.sb([128, 512], F32, f"wst{i}") for i in range(3)]
    wb = load_weight_bf16(k, w, K_, CS, "wb", stage)
    xb = [k.sb([128, KC, TT], BF16, f"xb{i}") for i in range(2)]
    ysb = [k.sb([128, MC, TT], F32, f"ysb{i}") for i in range(2)]
    ysq = [k.sb([128, MC, TT], F32, f"ysq{i}") for i in range(2)]
    row = k.sb([1, TT], F32, "row")
    pb = [k.ps(f"pb{i}") for i in range(4)]
    pss = k.ps("pss")
    xv = xT.rearrange("(kc p) s -> p kc s", p=128)
    yv = yT.rearrange("(m p) s -> p m s", p=128)
    NT = S // TT

    def ld(t):
        for kc in range(0, KC, 4):
            k.load(xb[t % 2], xb[t % 2].ap[:, kc:kc + 4, :], xv[:, kc:kc + 4, t * TT:(t + 1) * TT])

    ld(0)
    for t in range(NT):
        if t + 1 < NT:
            ld(t + 1)
        x = xb[t % 2]
        y, q = ysb[t % 2], ysq[t % 2]
        for m in range(MC):
            p = pb[m]
            for kc in range(KC):
                k.mm(p, p.ap, wb, wb.ap[:, kc, m * 128:(m + 1) * 128], x, x.ap[:, kc, :], kc == 0, kc == KC - 1)
            k.op("act", lambda: k.nc.scalar.copy(y.ap[:, m, :], p.ap), (p,), (y,))
            k.op("dve", lambda: k.nc.vector.tensor_tensor(q.ap[:, m, :], p.ap, y.ap[:, m, :], ALU.mult), (p, y), (q,))
        k.store(y, yv[:, :, t * TT:(t + 1) * TT], y.ap)
        ssq_out(k, c, [(q, q.ap[:, m, :]) for m in range(MC)], pss, row, ssq, t * TT, TT)
    return k.finish()


def build_prep(S):
    k = K()
    hT = k.dram("hT", [CS, S], F32, "ExternalInput")
    mT = k.dram("mT", [CS, S], F32, "ExternalInput")
    ssq4 = k.dram("ssq4", [NG, S], F32, "ExternalInput")
    wpost = k.dram("wpost", [128, MC], F32, "ExternalInput")
    wnext = k.dram("wnext", [128, MC], F32, "ExternalInput")
    hnT = k.dram("hnT", [CS, S], F32, "ExternalOutput")
    hbT = k.dram("hbT", [CS, S], BF16, "ExternalOutput")
    ssq = k.dram("ssq", [1, S], F32, "ExternalOutput")
    c = consts(k)
    wp = k.sb([128, MC], F32, "wp")
    wn = k.sb([128, MC], F32, "wn")
    k.load(wp, wp.ap, wpost)
    k.load(wn, wn.ap, wnext)
    hb_ = [k.sb([128, MC, TT], F32, f"h{i}") for i in range(2)]
    mb_ = [k.sb([128, MC, TT], F32, f"m{i}") for i in range(2)]
    ob_ = [k.sb([128, MC, TT], BF16, f"o{i}") for i in range(2)]
    sq_ = [k.sb([128, MC, TT], F32, f"q{i}") for i in range(2)]
    s4 = k.sb([NG, TT], F32, "s4")
    rs = k.sb([128, TT], F32, "rs")
    row = k.sb([1, TT], F32, "row")
    prs, pss = k.ps("prs"), k.ps("pss")
    hv = hT.rearrange("(m p) s -> p m s", p=128)
    mv = mT.rearrange("(m p) s -> p m s", p=128)
    hnv = hnT.rearrange("(m p) s -> p m s", p=128)
    hbv = hbT.rearrange("(m p) s -> p m s", p=128)
    NT = S // TT

    def ld(t):
        k.load(hb_[t % 2], hb_[t % 2].ap, hv[:, :, t * TT:(t + 1) * TT])
        k.load(mb_[t % 2], mb_[t % 2].ap, mv[:, :, t * TT:(t + 1) * TT])

    ld(0)
    for t in range(NT):
        if t + 1 < NT:
            ld(t + 1)
        h, m_, o, q = hb_[t % 2], mb_[t % 2], ob_[t % 2], sq_[t % 2]
        rstd_from_parts(k, c, ssq4, t * TT, TT, s4, prs, rs, float(D_MODEL))
        for m in range(MC):
            k.op("dve", lambda: k.nc.vector.tensor_tensor(m_.ap[:, m, :], m_.ap[:, m, :], rs.ap, ALU.mult), (m_, rs), (m_,))
            k.op("dve", lambda: k.nc.vector.scalar_tensor_tensor(h.ap[:, m, :], m_.ap[:, m, :], wp.ap[:, m:m + 1], h.ap[:, m, :],
                                                                 ALU.mult, ALU.add), (m_, wp, h), (h,))
            k.op("dve", lambda: k.nc.vector.tensor_scalar(o.ap[:, m, :], h.ap[:, m, :], wn.ap[:, m:m + 1], None, ALU.mult),
                 (h, wn), (o,))
            k.op("act", lambda: k.nc.scalar.activation(q.ap[:, m, :], h.ap[:, m, :], AF.Square), (h,), (q,))
        k.store(h, hnv[:, :, t * TT:(t + 1) * TT], h.ap)
        k.store(o, hbv[:, :, t * TT:(t + 1) * TT], o.ap)
        ssq_out(k, c, [(q, q.ap[:, m, :]) for m in range(MC)], pss, row, ssq, t * TT, TT)
    return k.finish()


_CACHE = {}


def _get(name, fn, *args):
    key = (name,) + args
    if key not in _CACHE:
        _CACHE[key] = fn(*args)
    return _CACHE[key]


def run(nc, in_maps):
    res = run_bass_kernel_spmd(nc, in_maps, core_ids=list(range(8)))
    return res.results


def build_ple(S):
    k = K()
    KC = D_MODEL // 128
    hbT = k.dram("hbT", [D_MODEL, S], BF16, "ExternalInput")
    hT = k.dram("hT", [CS, S], F32, "ExternalInput")
    pT = k.dram("pT", [PLE, S], F32, "ExternalInput")
    wpe = k.dram("wpe", [PLE, CS], F32, "ExternalInput")
    wpg = k.dram("wpg", [D_MODEL, CS], F32, "ExternalInput")
    wnext = k.dram("wnext", [128, MC], F32, "ExternalInput")
    hnT = k.dram("hnT", [CS, S], F32, "ExternalOutput")
    hwT = k.dram("hwT", [CS, S], BF16, "ExternalOutput")
    ssq = k.dram("ssq", [1, S], F32, "ExternalOutput")
    c = consts(k)
    stage = [k.sb([128, 512], F32, f"wst{i}") for i in range(3)]
    wg = load_weight_bf16(k, wpg, D_MODEL, CS, "wg", stage)
    we = load_weight_bf16(k, wpe, PLE, CS, "we", stage)
    wn = k.sb([128, MC], F32, "wn")
    k.load(wn, wn.ap, wnext)
    xb = [k.sb([128, KC, TT], BF16, f"xb{i}") for i in range(2)]
    pf = [k.sb([128, 2, TT], F32, f"pf{i}") for i in range(2)]
    pb16 = k.sb([128, 2, TT], BF16, "pb16")
    hb_ = [k.sb([128, MC, TT], F32, f"h{i}") for i in range(2)]
    ob_ = [k.sb([128, MC, TT], BF16, f"o{i}") for i in range(2)]
    sq_ = [k.sb([128, MC, TT], F32, f"q{i}") for i in range(2)]
    sg = [k.sb([128, TT], F32, f"sg{i}") for i in range(2)]
    row = k.sb([1, TT], F32, "row")
    pg = [k.ps(f"pg{i}") for i in range(2)]
    pe_ = [k.ps(f"pe{i}") for i in range(2)]
    pss = k.ps("pss")
    xv = hbT.rearrange("(kc p) s -> p kc s", p=128)
    pv = pT.rearrange("(kc p) s -> p kc s", p=128)
    hv = hT.rearrange("(m p) s -> p m s", p=128)
    hnv = hnT.rearrange("(m p) s -> p m s", p=128)
    hwv = hwT.rearrange("(m p) s -> p m s", p=128)
    NT = S // TT

    def ld(t):
        for kc in range(0, KC, 4):
            k.load(xb[t % 2], xb[t % 2].ap[:, kc:kc + 4, :], xv[:, kc:kc + 4, t * TT:(t + 1) * TT])
        k.load(pf[t % 2], pf[t % 2].ap, pv[:, :, t * TT:(t + 1) * TT])
        k.load(hb_[t % 2], hb_[t % 2].ap, hv[:, :, t * TT:(t + 1) * TT])

    ld(0)
    for t in range(NT):
        if t + 1 < NT:
            ld(t + 1)
        x, pp, h, o, q = xb[t % 2], pf[t % 2], hb_[t % 2], ob_[t % 2], sq_[t % 2]
        k.op("dve", lambda: k.nc.vector.tensor_copy(pb16.ap, pp.ap), (pp,), (pb16,))
        for m in range(MC):
            g_, e_, s_ = pg[m % 2], pe_[m % 2], sg[m % 2]
            for kc in range(KC):
                k.mm(g_, g_.ap, wg, wg.ap[:, kc, m * 128:(m + 1) * 128], x, x.ap[:, kc, :], kc == 0, kc == KC - 1)
            for kc in range(2):
                k.mm(e_, e_.ap, we, we.ap[:, kc, m * 128:(m + 1) * 128], pb16, pb16.ap[:, kc, :], kc == 0, kc == 1)
            k.op("act", lambda: k.nc.scalar.activation(s_.ap, g_.ap, AF.Sigmoid), (g_,), (s_,))
            k.op("dve", lambda: k.nc.vector.tensor_tensor(s_.ap, e_.ap, s_.ap, ALU.mult), (e_, s_), (s_,))
            k.op("dve", lambda: k.nc.vector.tensor_tensor(h.ap[:, m, :], h.ap[:, m, :], s_.ap, ALU.add), (h, s_), (h,))
            k.op("dve", lambda: k.nc.vector.tensor_scalar(o.ap[:, m, :], h.ap[:, m, :], wn.ap[:, m:m + 1], None, ALU.mult),
                 (h, wn), (o,))
            k.op("act", lambda: k.nc.scalar.activation(q.ap[:, m, :], h.ap[:, m, :], AF.Square), (h,), (q,))
        k.store(h, hnv[:, :, t * TT:(t + 1) * TT], h.ap)
        k.store(o, hwv[:, :, t * TT:(t + 1) * TT], o.ap)
        ssq_out(k, c, [(q, q.ap[:, m, :]) for m in range(MC)], pss, row, ssq, t * TT, TT)
    return k.finish()


def build_ffn_up(S, NSEQ=2):
    k = K()
    KC = D_MODEL // 128
    NJ = (FG + 127) // 128
    rows = [min(128, FG - j * 128) for j in range(NJ)]
    xT = k.dram("xT", [D_MODEL, S], BF16, "ExternalInput")
    ssq4 = k.dram("ssq4", [NG, S], F32, "ExternalInput")
    w = k.dram("w", [D_MODEL, 2 * FG], F32, "ExternalInput")
    cw = k.dram("cw", [128, 2 * NJ, 3], F32, "ExternalInput")
    cb = k.dram("cb", [128, 2 * NJ], F32, "ExternalInput")
    gT = k.dram("gT", [FG, S], BF16, "ExternalOutput")
    c = consts(k)
    stage = [k.sb([128, 512], F32, f"wst{i}") for i in range(3)]
    wb = load_weight_bf16(k, w, D_MODEL, 2 * FG, "wb", stage)
    cwt = k.sb([128, 2 * NJ, 3], F32, "cwt")
    cbt = k.sb([128, 2 * NJ], F32, "cbt")
    k.load(cwt, cwt.ap, cw)
    k.load(cbt, cbt.ap, cb)
    halo = k.sb([128, 2 * NJ, 2], F32, "halo")
    xb = [k.sb([128, KC, TT], BF16, f"xb{i}") for i in range(2)]
    s4 = k.sb([NG, TT], F32, "s4")
    rs = k.sb([128, TT], F32, "rs")
    up = [k.sb([128, TT + 2], F32, f"up{i}") for i in range(2)]
    acc = [k.sb([128, TT], F32, f"acc{i}") for i in range(2)]
    t1 = k.sb([128, TT], F32, "t1")
    t2 = k.sb([128, TT], F32, "t2")
    gout = [k.sb([128, NJ, TT], BF16, f"gout{i}") for i in range(2)]
    pu = [k.ps(f"pu{i}") for i in range(4)]
    prs = k.ps("prs")
    xv = xT.rearrange("(kc p) s -> p kc s", p=128)
    NF = NJ - 1 if rows[-1] < 128 else NJ
    gv = gT[0:NF * 128, :].rearrange("(j p) s -> p j s", p=128)
    NT = S // TT
    NTS = NT // NSEQ

    def ld(t):
        for kc in range(0, KC, 4):
            k.load(xb[t % 2], xb[t % 2].ap[:, kc:kc + 4, :], xv[:, kc:kc + 4, t * TT:(t + 1) * TT])

    ld(0)
    ip = 0
    for t in range(NT):
        if t + 1 < NT:
            ld(t + 1)
        if t % NTS == 0:
            k.op("dve", lambda: k.nc.vector.memset(halo.ap, 0.0), (), (halo,))
        x, go = xb[t % 2], gout[t % 2]
        rstd_from_parts(k, c, ssq4, t * TT, TT, s4, prs, rs, float(D_MODEL))
        for j in range(NJ):
            R = rows[j]
            for wh in range(2):
                ct = j + wh * NJ
                c0 = wh * FG + j * 128
                p = pu[ip % 4]
                ip += 1
                u, a = up[wh], acc[wh]
                for kc in range(KC):
                    k.mm(p, p.ap[0:R, :], wb, wb.ap[:, kc, c0:c0 + R], x, x.ap[:, kc, :], kc == 0, kc == KC - 1)
                k.op("dve", lambda: k.nc.vector.tensor_copy(u.ap[0:R, 0:2], halo.ap[0:R, ct, :]), (halo,), (u,))
                k.op("dve", lambda: k.nc.vector.tensor_tensor(u.ap[0:R, 2:TT + 2], p.ap[0:R, :], rs.ap[0:R, :], ALU.mult), (p, rs), (u,))
                k.op("dve", lambda: k.nc.vector.tensor_copy(halo.ap[0:R, ct, :], u.ap[0:R, TT:TT + 2]), (u,), (halo,))
                k.op("act", lambda: k.nc.scalar.activation(a.ap[0:R, :], u.ap[0:R, 2:TT + 2], AF.Identity, bias=cbt.ap[0:R, ct:ct + 1],
                                                           scale=cwt.ap[0:R, ct, 2:3]), (u, cbt, cwt), (a,))
                k.op("dve", lambda: k.nc.vector.scalar_tensor_tensor(a.ap[0:R, :], u.ap[0:R, 1:TT + 1], cwt.ap[0:R, ct, 1:2], a.ap[0:R, :],
                                                                     ALU.mult, ALU.add), (u, cwt, a), (a,))
                k.op("dve", lambda: k.nc.vector.scalar_tensor_tensor(a.ap[0:R, :], u.ap[0:R, 0:TT], cwt.ap[0:R, ct, 0:1], a.ap[0:R, :],
                                                                     ALU.mult, ALU.add), (u, cwt, a), (a,))
            ga, va = acc[0], acc[1]
            k.op("act", lambda: k.nc.scalar.activation(t1.ap[0:R, :], ga.ap[0:R, :], AF.Square), (ga,), (t1,))
            k.op("dve", lambda: k.nc.vector.tensor_scalar(t1.ap[0:R, :], t1.ap[0:R, :], 0.044715, 1.0, ALU.mult, ALU.add), (t1,), (t1,))
            k.op("dve", lambda: k.nc.vector.tensor_tensor(t1.ap[0:R, :], t1.ap[0:R, :], ga.ap[0:R, :], ALU.mult), (t1, ga), (t1,))
            k.op("act", lambda: k.nc.scalar.activation(t2.ap[0:R, :], t1.ap[0:R, :], AF.Sigmoid, scale=1.5957691216057308), (t1,), (t2,))
            k.op("dve", lambda: k.nc.vector.tensor_tensor(t2.ap[0:R, :], t2.ap[0:R, :], ga.ap[0:R, :], ALU.mult), (t2, ga), (t2,))
            k.op("dve", lambda: k.nc.vector.tensor_tensor(go.ap[0:R, j, :], t2.ap[0:R, :], va.ap[0:R, :], ALU.mult), (t2, va), (go,))
        k.store(go, gv[:, :, t * TT:(t + 1) * TT], go.ap[:, 0:NF, :])
        if NF < NJ:
            k.store(go, gT[NF * 128:FG, t * TT:(t + 1) * TT], go.ap[0:rows[-1], NF, :])
    return k.finish()


def full_barrier(k, tiles):
    engs = ["pe", "act", "dve", "pool", "sp"]
    for e in engs:
        for o in engs:
            if o != e and k.cnt[o] > 0 and k.waited[e].get(o, 0) < k.cnt[o]:
                k.h[e].wait_ge(k.sem[o], k.cnt[o])
                k.waited[e][o] = k.cnt[o]
        for t in tiles:
            if t.dsem is not None and t.dcount > 0 and k.waited[e].get("d_" + t.name, 0) < t.dcount:
                k.h[e].wait_ge(t.dsem, t.dcount)
                k.waited[e]["d_" + t.name] = t.dcount


def mixer_consts():
    s = np.arange(128)
    same = (s[:, None] // 64) == (s[None, :] // 64)
    le = s[:, None] <= s[None, :]
    mid = (s // 64) * 64 + 31
    D = same * (le.astype(np.float32) - (s[:, None] <= mid[None, :]).astype(np.float32))
    sel = np.zeros((128, 128), np.float32)
    sel[:, 0] = s <= 31
    sel[:, 1] = s < 64
    sel[:, 2] = (s >= 64) & (s <= 95)
    sel[:, 3] = s >= 64
    suf = (same & (s[:, None] > s[None, :])).astype(np.float32)
    cf = np.concatenate([D, -D, sel, suf], 1).astype(np.float32)
    tri = (same & le).astype(np.float32)
    mask2 = np.concatenate([(s[:, None] >= s[None, :]), (s[:, None] <= s[None, :])], 1).astype(np.float32)
    ident = np.eye(128, dtype=np.float32)
    cb = np.concatenate([tri, mask2, ident], 1).astype(ml_dtypes.bfloat16)
    return cf, cb


def build_mixer(S, dbg="ha", NSEQ=2):
    k = K()
    nc = k.nc
    KC = D_MODEL // 128
    xT = k.dram("xT", [D_MODEL, S], BF16, "ExternalInput")
    ssq4 = k.dram("ssq4", [NG, S], F32, "ExternalInput")
    wfm = k.dram("wfm", [D_MODEL, 768], F32, "ExternalInput")
    wtm = k.dram("wtm", [D_MODEL, 256], F32, "ExternalInput")
    lbl_fm = k.dram("lbl_fm", [128, 2, 4], F32, "ExternalInput")
    lbl_tm = k.dram("lbl_tm", [128, 4, 128], F32, "ExternalInput")
    lmask = k.dram("lmask", [128, 4], F32, "ExternalInput")
    hnorm = k.dram("hnorm", [128, 2], F32, "ExternalInput")
    cf_d = k.dram("cf", [128, 512], F32, "ExternalInput")
    cb_d = k.dram("cb", [128, 512], BF16, "ExternalInput")
    oT = k.dram("oT", [CS, S], BF16, "ExternalOutput")
    fmS = k.dram("fmS", [768, S], BF16, "Internal")
    logfS = k.dram("logfS", [S, 128], F32, "Internal")
    kftmS = k.dram("kftmS", [S, 128], BF16, "Internal")
    vtmS = k.dram("vtmS", [S, 128], BF16, "Internal")

    c = consts(k)
    cf = k.sb([128, 512], F32, "cf_sb")
    cb = k.sb([128, 512], BF16, "cb_sb")
    k.load(cf, cf.ap, cf_d)
    k.load(cb, cb.ap, cb_d)
    hn = k.sb([128, 2], F32, "hn")
    k.load(hn, hn.ap, hnorm)
    PB = [k.ps(f"pb{i}") for i in range(7)]
    PT = k.ps("ptr", shape=(128, 1024), dt=BF16)

    lm = k.sb([128, 4], F32, "lm")
    k.load(lm, lm.ap, lmask)
    lf_ = k.sb([128, 2, 4], F32, "lblf")
    lt_ = k.sb([128, 4, 128], F32, "lblt")
    k.load(lf_, lf_.ap, lbl_fm)
    k.load(lt_, lt_.ap, lbl_tm)
    lb_fm = k.sb([128, 2], F32, "lb_fm")
    oml_fm = k.sb([128, 2], F32, "oml_fm")
    lb_tm = k.sb([128, 128], F32, "lb_tm")
    oml_tm = k.sb([128, 128], F32, "oml_tm")
    tmpa = k.sb([128, 128], F32, "tmpa")
    tmpb = k.sb([128, 128], F32, "tmpb")
    V = nc.vector
    k.op("dve", lambda: V.tensor_tensor(tmpa.ap, lt_.ap[:, 0, :], lt_.ap[:, 1, :], ALU.max), (lt_,), (tmpa,))
    k.op("dve", lambda: V.tensor_tensor(tmpa.ap, tmpa.ap, lt_.ap[:, 2, :], ALU.max), (lt_, tmpa), (tmpa,))
    k.op("dve", lambda: V.tensor_tensor(tmpa.ap, tmpa.ap, lt_.ap[:, 3, :], ALU.max), (lt_, tmpa), (tmpa,))
    for j in range(4):
        k.op("dve", lambda: V.tensor_tensor(lt_.ap[:, j, :], lt_.ap[:, j, :], tmpa.ap, ALU.subtract), (lt_, tmpa), (lt_,))
    k.op("act", lambda: nc.scalar.activation(lt_.ap, lt_.ap, AF.Exp), (lt_,), (lt_,))
    k.op("dve", lambda: V.tensor_tensor(tmpa.ap, lt_.ap[:, 0, :], lt_.ap[:, 1, :], ALU.add), (lt_,), (tmpa,))
    k.op("dve", lambda: V.tensor_tensor(tmpa.ap, tmpa.ap, lt_.ap[:, 2, :], ALU.add), (lt_, tmpa), (tmpa,))
    k.op("dve", lambda: V.tensor_tensor(tmpa.ap, tmpa.ap, lt_.ap[:, 3, :], ALU.add), (lt_, tmpa), (tmpa,))
    k.op("dve", lambda: V.reciprocal(tmpa.ap, tmpa.ap), (tmpa,), (tmpa,))
    k.op("dve", lambda: V.tensor_scalar(tmpb.ap, lt_.ap[:, 0, :], lm.ap[:, 0:1], None, ALU.mult), (lt_, lm), (tmpb,))
    for j in range(1, 4):
        k.op("dve", lambda: V.scalar_tensor_tensor(tmpb.ap, lt_.ap[:, j, :], lm.ap[:, j:j + 1], tmpb.ap, ALU.mult, ALU.add),
             (lt_, lm, tmpb), (tmpb,))
    k.op("dve", lambda: V.tensor_tensor(lb_tm.ap, tmpb.ap, tmpa.ap, ALU.mult), (tmpa, tmpb), (lb_tm,))
    k.op("dve", lambda: V.tensor_scalar(oml_tm.ap, lb_tm.ap, -1.0, 1.0, ALU.mult, ALU.add), (lb_tm,), (oml_tm,))
    s2a = k.sb([128, 2], F32, "s2a")
    s2b = k.sb([128, 2], F32, "s2b")
    k.op("dve", lambda: V.tensor_tensor(s2a.ap, lf_.ap[:, :, 0], lf_.ap[:, :, 1], ALU.max), (lf_,), (s2a,))
    k.op("dve", lambda: V.tensor_tensor(s2a.ap, s2a.ap, lf_.ap[:, :, 2], ALU.max), (lf_, s2a), (s2a,))
    k.op("dve", lambda: V.tensor_tensor(s2a.ap, s2a.ap, lf_.ap[:, :, 3], ALU.max), (lf_, s2a), (s2a,))
    for j in range(4):
        k.op("dve", lambda: V.tensor_tensor(lf_.ap[:, :, j], lf_.ap[:, :, j], s2a.ap, ALU.subtract), (lf_, s2a), (lf_,))
    k.op("act", lambda: nc.scalar.activation(lf_.ap, lf_.ap, AF.Exp), (lf_,), (lf_,))
    k.op("dve", lambda: V.tensor_tensor(s2a.ap, lf_.ap[:, :, 0], lf_.ap[:, :, 1], ALU.add), (lf_,), (s2a,))
    k.op("dve", lambda: V.tensor_tensor(s2a.ap, s2a.ap, lf_.ap[:, :, 2], ALU.add), (lf_, s2a), (s2a,))
    k.op("dve", lambda: V.tensor_tensor(s2a.ap, s2a.ap, lf_.ap[:, :, 3], ALU.add), (lf_, s2a), (s2a,))
    k.op("dve", lambda: V.reciprocal(s2a.ap, s2a.ap), (s2a,), (s2a,))
    k.op("dve", lambda: V.tensor_scalar(s2b.ap, lf_.ap[:, :, 0], lm.ap[:, 0:1], None, ALU.mult), (lf_, lm), (s2b,))
    for j in range(1, 4):
        k.op("dve", lambda: V.scalar_tensor_tensor(s2b.ap, lf_.ap[:, :, j], lm.ap[:, j:j + 1], s2b.ap, ALU.mult, ALU.add),
             (lf_, lm, s2b), (s2b,))
    k.op("dve", lambda: V.tensor_tensor(lb_fm.ap, s2b.ap, s2a.ap, ALU.mult), (s2a, s2b), (lb_fm,))
    k.op("dve", lambda: V.tensor_scalar(oml_fm.ap, lb_fm.ap, -1.0, 1.0, ALU.mult, ALU.add), (lb_fm,), (oml_fm,))

    from contextlib import ExitStack
    p1_tiles = []
    with ExitStack() as es:
        def sbs(shape, dt, name):
            hnd = es.enter_context(nc.sbuf_tensor(name, list(shape), dt))
            t = T(k, hnd.ap(), name)
            p1_tiles.append(t)
            return t

        stage = [sbs([128, 512], F32, f"wst{i}") for i in range(3)]
        def ldw(w_dram, N_, name):
            wb = sbs([128, KC, N_], BF16, name)
            wv = w_dram.rearrange("(kc p) n -> p kc n", p=128)
            i = 0
            for kc in range(KC):
                for n0 in range(0, N_, 256):
                    st = stage[i % 3]
                    k.load(st, st.ap[:, 0:256], wv[:, kc, n0:n0 + 256])
                    if i % 2 == 0:
                        k.op("act", lambda: nc.scalar.copy(wb.ap[:, kc, n0:n0 + 256], st.ap[:, 0:256]), (st,), (wb,))
                    else:
                        k.op("dve", lambda: V.tensor_copy(wb.ap[:, kc, n0:n0 + 256], st.ap[:, 0:256]), (st,), (wb,))
                    i += 1
            return wb

        wf = ldw(wfm, 768, "wf")
        wt = ldw(wtm, 256, "wt")
        xb = [sbs([128, KC, TT], BF16, f"xb{i}") for i in range(2)]
        s4 = sbs([NG, TT], F32, "s4")
        rs = sbs([128, TT], F32, "rs")
        rt = sbs([128, 4], F32, "rt")
        fo = [sbs([128, 6, TT], BF16, f"fo{i}") for i in range(2)]
        u1 = [sbs([128, TT], F32, f"u1_{i}") for i in range(2)]
        lo = [sbs([128, 4, 128], F32, f"lo{i}") for i in range(2)]
        ko = [sbs([128, 4, 128], BF16, f"ko{i}") for i in range(2)]
        vo = [sbs([128, 4, 128], BF16, f"vo{i}") for i in range(2)]
        sg = [sbs([128, 128], F32, f"sg{i}") for i in range(2)]
        xv = xT.rearrange("(kc p) s -> p kc s", p=128)
        fmv = fmS.rearrange("(ct p) s -> p ct s", p=128)
        NT = S // TT

        def ld(t):
            for kc in range(0, KC, 4):
                k.load(xb[t % 2], xb[t % 2].ap[:, kc:kc + 4, :], xv[:, kc:kc + 4, t * TT:(t + 1) * TT])

        ld(0)
        ip = 0
        for t in range(NT):
            if t + 1 < NT:
                ld(t + 1)
            x, f_o = xb[t % 2], fo[t % 2]
            rstd_from_parts(k, c, ssq4, t * TT, TT, s4, PB[6], rs, float(D_MODEL))
            for i in range(4):
                k.mm(PB[5], PB[5].ap[:, i * 128:(i + 1) * 128], s4, s4.ap[:, i * 128:(i + 1) * 128], c["ones_f"], c["ones_f"].ap[0:NG, :], True, True)
            k.op("act", lambda: nc.scalar.activation(rt.ap, PB[5].ap[:, 0:512:128], AF.Sqrt, bias=c["eps"].ap[:, 0:1], scale=1.0 / D_MODEL),
                 (PB[5], c["eps"]), (rt,))
            k.op("dve", lambda: V.reciprocal(rt.ap, rt.ap), (rt,), (rt,))
            for ct in (range(6) if "F" not in dbg else []):
                slot, hd = ct, 0
                p = PB[ip % 3]
                ip += 1
                for kc in range(KC):
                    k.mm(p, p.ap, wf, wf.ap[:, kc, ct * 128:(ct + 1) * 128], x, x.ap[:, kc, :], kc == 0, kc == KC - 1)
                if slot in (0, 2):
                    u = u1[ct % 2]
                    k.op("dve", lambda: V.tensor_tensor(u.ap, p.ap, rs.ap, ALU.mult), (p, rs), (u,))
                    k.op("act", lambda: nc.scalar.activation(f_o.ap[:, ct, :], u.ap, AF.Silu), (u,), (f_o,))
                elif slot == 1:
                    u = u1[ct % 2]
                    k.op("dve", lambda: V.tensor_tensor(u.ap, p.ap, rs.ap, ALU.mult), (p, rs), (u,))
                    k.op("act", lambda: nc.scalar.activation(u.ap, u.ap, AF.Sigmoid, scale=-1.0), (u,), (u,))
                    k.op("dve", lambda: V.tensor_scalar(f_o.ap[:, ct, :], u.ap, oml_fm.ap[:, hd:hd + 1], None, ALU.mult),
                         (u, oml_fm), (f_o,))
                else:
                    k.op("dve", lambda: V.tensor_tensor(f_o.ap[:, ct, :], p.ap, rs.ap, ALU.mult), (p, rs), (f_o,))
            for c4 in range(0, 6, 3):
                k.store(f_o, fmv[:, c4:c4 + 3, t * TT:(t + 1) * TT], f_o.ap[:, c4:c4 + 3, :])
            l_o, k_o, v_o = lo[t % 2], ko[t % 2], vo[t % 2]
            for i in (range(4) if "T" not in dbg else []):
                p = PB[3 + (i % 2)]
                s_ = sg[i % 2]
                for kc in range(KC):
                    k.mm(p, p.ap[:, 0:256], x, x.ap[:, kc, i * 128:(i + 1) * 128], wt, wt.ap[:, kc, :], kc == 0, kc == KC - 1)
                k.op("act", lambda: nc.scalar.activation(s_.ap, p.ap[:, 0:128], AF.Sigmoid, scale=rt.ap[:, i:i + 1]), (p, rt), (s_,))
                k.op("dve", lambda: V.tensor_tensor(s_.ap, s_.ap, oml_tm.ap, ALU.mult), (s_, oml_tm), (s_,))
                k.op("dve", lambda: V.tensor_tensor(s_.ap, s_.ap, lb_tm.ap, ALU.add), (s_, lb_tm), (s_,))
                k.op("act", lambda: nc.scalar.activation(l_o.ap[:, i, :], s_.ap, AF.Ln), (s_,), (l_o,))
                k.op("dve", lambda: V.tensor_scalar(k_o.ap[:, i, :], s_.ap, -1.0, 1.0, ALU.mult, ALU.add), (s_,), (k_o,))
                k.op("act", lambda: nc.scalar.activation(v_o.ap[:, i, :], p.ap[:, 128:256], AF.Identity, scale=rt.ap[:, i:i + 1]),
                     (p, rt), (v_o,))
            tv = lambda d_: d_[t * TT:(t + 1) * TT, :].rearrange("(i p) c -> p i c", p=128)
            k.store(l_o, tv(logfS), l_o.ap, final=True)
            k.store(k_o, tv(kftmS), k_o.ap, final=True)
            k.store(v_o, tv(vtmS), v_o.ap, final=True)
        full_barrier(k, p1_tiles)
    k.out_dmas = []

    if "h" not in dbg and "a" not in dbg:
        return k.finish()
    TB = 2048
    SSEQ = S // NSEQ
    NB = SSEQ // TB
    E_SCALE = 128 ** -0.5
    qf = k.sb([128, TB], BF16, "qf")
    kff = k.sb([128, TB], BF16, "kff")
    gf = k.sb([128, TB], BF16, "gf")
    aqf = k.sb([128, TB], BF16, "aqf")
    avf = k.sb([128, TB], BF16, "avf")
    akf = [[k.sb([128, TB], BF16, f"akf{h}_{s}") for s in range(2)] for h in range(1)]
    lgf = k.sb([128, 16, 128], F32, "lgf")
    kft = k.sb([128, 16, 128], BF16, "kft")
    vt = k.sb([128, 16, 128], BF16, "vt")
    Sst = [k.sb([128, 128], F32, f"Sst{h}") for h in range(1)]
    Sbf = [k.sb([128, 128], BF16, f"Sbf{i}") for i in range(2)]
    e12 = [k.sb([128, 256], F32, f"e12_{i}") for i in range(2)]
    e3 = [k.sb([128, 128], F32, f"e3_{i}") for i in range(2)]
    esl = [k.sb([128, 4], F32, f"esl{i}") for i in range(2)]
    qt_ = [k.sb([128, 128], BF16, f"qt{i}") for i in range(2)]
    kt_ = [k.sb([128, 128], BF16, f"kt{i}") for i in range(2)]
    kh_ = [k.sb([128, 2, 128], BF16, f"kh{i}") for i in range(2)]
    sc_ = [k.sb([128, 128], BF16, f"sc{i}") for i in range(2)]
    oraw = [k.sb([128, 512], F32, f"oraw{i}") for i in range(2)]
    osq = [k.sb([128, 512], BF16, f"osq{i}") for i in range(2)]
    rr = k.sb([128, 512], F32, "rr")
    orec = k.sb([128, TB], BF16, "orec")
    oatt = k.sb([128, TB], BF16, "oatt")
    vtu = k.sb([128, 48, 128], BF16, "vtu")
    vcar = [k.sb([128, 21, 128], BF16, f"vcar{h}") for h in range(1)]
    pT = [k.sb([128, 256], BF16, f"pT{i}") for i in range(2)]
    UZ = k.sb([128, 2, TB], F32, "UZ")
    A = nc.scalar
    tri = cb.ap[:, 0:128]
    mask2 = cb.ap[:, 128:384]
    ident = cb.ap[:, 384:512]
    pA, pB_, pC, pD, pE, pF, pG = PB

    for sq in range(NSEQ):
        for B in range(NB):
            hd = 0
            t0 = sq * SSEQ + B * TB
            if B == 0:
                k.op("dve", lambda: V.memset(Sst[0].ap, 0.0), (), (Sst[0],))
            cur, prv = akf[hd][B % 2], akf[hd][(B + 1) % 2]
            for (tl, slot) in ((qf, 0), (kff, 1), (gf, 2), (aqf, 3), (cur, 4), (avf, 5)):
                r0 = slot * 128
                k.load(tl, tl.ap, fmS[r0:r0 + 128, t0:t0 + TB])
            tmv = lambda d_: d_[t0:t0 + TB, :].rearrange("(i p) c -> p i c", p=128)
            for i4 in range(0, 16, 4):
                k.load(lgf, lgf.ap[:, i4:i4 + 4, :], tmv(logfS)[:, i4:i4 + 4, :])
                k.load(kft, kft.ap[:, i4:i4 + 4, :], tmv(kftmS)[:, i4:i4 + 4, :])
                k.load(vt, vt.ap[:, i4:i4 + 4, :], tmv(vtmS)[:, i4:i4 + 4, :])
            S_ = Sst[hd]
            for i in (range(16) if "h" in dbg else []):
                a = i % 2
                lf = lgf.ap[:, i, :]
                tsl = slice(i * 128, (i + 1) * 128)
                k.mm(pA, pA.ap[:, 0:256], lgf, lf, cf, cf.ap[:, 0:256], True, True)
                k.mm(pA, pA.ap[:, 256:384], lgf, lf, cf, cf.ap[:, 256:384], True, True)
                k.mm(pB_, pB_.ap[:, 0:128], cf, cf.ap[:, 384:512], lgf, lf, True, True)
                k.op("act", lambda: A.activation(e12[a].ap, pA.ap[:, 0:256], AF.Exp), (pA,), (e12[a],))
                k.op("act", lambda: A.activation(esl[a].ap, pA.ap[:, 256:260], AF.Exp), (pA,), (esl[a],))
                k.op("act", lambda: A.activation(e3[a].ap, pB_.ap[:, 0:128], AF.Exp), (pB_,), (e3[a],))
                k.op("dve", lambda: V.tensor_tensor(qt_[a].ap, qf.ap[:, tsl], e12[a].ap[:, 0:128], ALU.mult), (qf, e12[a]), (qt_[a],))
                k.op("dve", lambda: V.tensor_tensor(kt_[a].ap, kff.ap[:, tsl], e12[a].ap[:, 128:256], ALU.mult), (kff, e12[a]), (kt_[a],))
                for ch in range(2):
                    k.op("dve", lambda: V.scalar_tensor_tensor(kh_[a].ap[:, ch, :], kft.ap[:, i, :], cf.ap[:, 257 + 2 * ch:258 + 2 * ch], e3[a].ap,
                                                               ALU.mult, ALU.mult), (kft, cf, e3[a]), (kh_[a],))
                if "1" in dbg:
                    continue
                k.mm(pC, pC.ap[:, 0:128], kt_[a], kt_[a].ap, qt_[a], qt_[a].ap, True, True)
                k.op("dve", lambda: V.tensor_tensor(sc_[a].ap, pC.ap[:, 0:128], tri, ALU.mult), (pC, cb), (sc_[a],))
                if "2" in dbg:
                    continue
                for ch in range(2):
                    k.mm(pD, pD.ap[:, ch * 128:(ch + 1) * 128], kh_[a], kh_[a].ap[:, ch, :], vt, vt.ap[:, i, :], True, True)
                k.op("dve", lambda: V.tensor_scalar(Sbf[0].ap, S_.ap, esl[a].ap[:, 0:1], None, ALU.mult), (S_, esl[a]), (Sbf[0],))
                k.op("dve", lambda: V.scalar_tensor_tensor(S_.ap, S_.ap, esl[a].ap[:, 1:2], pD.ap[:, 0:128], ALU.mult, ALU.add),
                     (S_, esl[a], pD), (S_,))
                k.op("dve", lambda: V.tensor_scalar(Sbf[1].ap, S_.ap, esl[a].ap[:, 2:3], None, ALU.mult), (S_, esl[a]), (Sbf[1],))
                k.op("dve", lambda: V.scalar_tensor_tensor(S_.ap, S_.ap, esl[a].ap[:, 3:4], pD.ap[:, 128:256], ALU.mult, ALU.add),
                     (S_, esl[a], pD), (S_,))
                if "3" in dbg:
                    continue
                k.mm(pE, pE.ap[:, 0:128], vt, vt.ap[:, i, :], sc_[a], sc_[a].ap, True, True)
                for ch in range(2):
                    k.mm(pE, pE.ap[:, 128 + ch * 64:128 + (ch + 1) * 64], Sbf[ch], Sbf[ch].ap, qt_[a], qt_[a].ap[:, ch * 64:(ch + 1) * 64],
                         True, True)
                ob = (i // 4) % 2
                osl = slice((i % 4) * 128, (i % 4 + 1) * 128)
                k.op("act", lambda: A.copy(oraw[ob].ap[:, osl], pE.ap[:, 0:128]), (pE,), (oraw[ob],))
                k.op("dve", lambda: V.tensor_tensor(oraw[ob].ap[:, osl], oraw[ob].ap[:, osl], pE.ap[:, 128:256], ALU.add),
                     (oraw[ob], pE), (oraw[ob],))
                k.op("act", lambda: A.activation(osq[ob].ap[:, osl], oraw[ob].ap[:, osl], AF.Square), (oraw[ob],), (osq[ob],))
                if i % 4 == 3:
                    g0 = (i // 4) * 512
                    k.mm(pF, pF.ap, c["ones_bf"], c["ones_bf"].ap, osq[ob], osq[ob].ap, True, True)
                    k.op("act", lambda: A.activation(rr.ap, pF.ap, AF.Sqrt, bias=c["eps"].ap[:, 0:1], scale=1.0 / 128), (pF, c["eps"]), (rr,))
                    k.op("dve", lambda: V.reciprocal(rr.ap, rr.ap), (rr,), (rr,))
                    k.op("dve", lambda: V.tensor_tensor(rr.ap, rr.ap, oraw[ob].ap, ALU.mult), (rr, oraw[ob]), (rr,))
                    k.op("dve", lambda: V.scalar_tensor_tensor(orec.ap[:, g0:g0 + 512], rr.ap, hn.ap[:, hd:hd + 1], gf.ap[:, g0:g0 + 512],
                                                               ALU.mult, ALU.mult), (rr, hn, gf), (orec,))
            k.store(orec, oT[0:128, t0:t0 + TB], orec.ap)
            if "a" not in dbg:
                continue
            for bi, d in enumerate((1, 4, 16)):
                R, U = d, 16 // d
                for u0 in range(0, 16, 4):
                    for uu in range(4):
                        u = u0 + uu
                        n_, r_ = u // R, u % R
                        st_ = n_ * 128 * d + r_
                        k.op("pe", lambda: nc.tensor.transpose(PT.ap[:, uu * 128:(uu + 1) * 128], avf.ap[:, st_:st_ + 127 * d + 1:d], ident),
                             (avf, cb), (PT,))
                    k.op("act", lambda: A.copy(vtu.ap[:, bi * 16 + u0:bi * 16 + u0 + 4, :],
                                               PT.ap[:, 0:512].rearrange("p (u e) -> p u e", u=4)), (PT,), (vtu,))
            iu = 0
            for bi, d in enumerate((1, 4, 16)):
                R, U = d, 16 // d
                coff = (0, 1, 5)[bi]
                for u in range(16):
                    n_, r_ = u // R, u % R
                    st_ = n_ * 128 * d + r_
                    qsl = slice(st_, st_ + 127 * d + 1, d)
                    has_prev = (n_ > 0) or (B > 0)
                    p_ = pT[iu % 2]
                    sps = pA if iu % 2 == 0 else pB_
                    ups = pC if iu % 2 == 0 else pD
                    iu += 1
                    k.mm(sps, sps.ap[:, 128:256], cur, cur.ap[:, qsl], aqf, aqf.ap[:, qsl], True, True)
                    if has_prev:
                        if n_ > 0:
                            pst = (n_ - 1) * 128 * d + r_
                            ksrc, vsrc_t, vsrc = cur, vtu, vtu.ap[:, bi * 16 + u - R, :]
                        else:
                            pst = (U - 1) * 128 * d + r_
                            ksrc, vsrc_t, vsrc = prv, vcar[hd], vcar[hd].ap[:, coff + r_, :]
                        k.mm(sps, sps.ap[:, 0:128], ksrc, ksrc.ap[:, pst:pst + 127 * d + 1:d], aqf, aqf.ap[:, qsl], True, True)
                        k.op("act", lambda: A.activation(p_.ap, sps.ap[:, 0:256], AF.Exp, scale=E_SCALE), (sps,), (p_,))
                        k.op("dve", lambda: V.tensor_tensor(p_.ap, p_.ap, mask2, ALU.mult), (p_, cb), (p_,))
                        k.mm(ups, ups.ap[:, 0:128], vsrc_t, vsrc, p_, p_.ap[:, 0:128], True, False)
                        k.mm(ups, ups.ap[:, 0:128], vtu, vtu.ap[:, bi * 16 + u, :], p_, p_.ap[:, 128:256], False, True)
                        k.mm(ups, ups.ap[:, 128:256], c["ones_bf"], c["ones_bf"].ap, p_, p_.ap[:, 0:128], True, False)
                        k.mm(ups, ups.ap[:, 128:256], c["ones_bf"], c["ones_bf"].ap, p_, p_.ap[:, 128:256], False, True)
                    else:
                        k.op("act", lambda: A.activation(p_.ap[:, 128:256], sps.ap[:, 128:256], AF.Exp, scale=E_SCALE), (sps,), (p_,))
                        k.op("dve", lambda: V.tensor_tensor(p_.ap[:, 128:256], p_.ap[:, 128:256], cb.ap[:, 256:384], ALU.mult), (p_, cb), (p_,))
                        k.mm(ups, ups.ap[:, 0:128], vtu, vtu.ap[:, bi * 16 + u, :], p_, p_.ap[:, 128:256], True, True)
                        k.mm(ups, ups.ap[:, 128:256], c["ones_bf"], c["ones_bf"].ap, p_, p_.ap[:, 128:256], True, True)
                    src = ups.ap[:, 0:256].rearrange("p (a q) -> p a q", a=2)
                    if bi == 0:
                        k.op("act", lambda: A.copy(UZ.ap[:, :, qsl], src), (ups,), (UZ,))
                    else:
                        k.op("dve", lambda: V.tensor_tensor(UZ.ap[:, :, qsl], UZ.ap[:, :, qsl], src, ALU.add), (UZ, ups), (UZ,))
            k.op("dve", lambda: V.tensor_copy(vcar[hd].ap[:, 0:1, :], vtu.ap[:, 15:16, :]), (vtu,), (vcar[hd],))
            k.op("dve", lambda: V.tensor_copy(vcar[hd].ap[:, 1:5, :], vtu.ap[:, 28:32, :]), (vtu,), (vcar[hd],))
            k.op("dve", lambda: V.tensor_copy(vcar[hd].ap[:, 5:21, :], vtu.ap[:, 32:48, :]), (vtu,), (vcar[hd],))
            k.op("dve", lambda: V.reciprocal(UZ.ap[:, 1, :], UZ.ap[:, 1, :]), (UZ,), (UZ,))
            k.op("dve", lambda: V.tensor_tensor(oatt.ap, UZ.ap[:, 0, :], UZ.ap[:, 1, :], ALU.mult), (UZ,), (oatt,))
            k.store(oatt, oT[128:256, t0:t0 + TB], oatt.ap)
    return k.finish()


def _vm(v):
    return np.ascontiguousarray(np.asarray(v, np.float32).reshape(MC, 128).T)


def kernel(x, p, ln_mix_pre, w_in, lb_logits, hgrn_norm, w_out, ln_mix_post, ln_ffn_pre, w_up, conv_w, conv_b,
           w_down, ln_ffn_post, w_pe, w_pg):
    x = np.asarray(x, np.float32)
    Bn, S, D = x.shape
    L = np.asarray(w_in).shape[0]
    assert D == D_MODEL
    S2 = Bn * S
    prep = _get("prep", build_prep, S2)
    mixer = _get("mixer", build_mixer, S2, "ha", Bn)
    d2k = _get("dense", build_dense, S2, D_MODEL)
    d5k = _get("dense", build_dense, S2, D_FF)
    upp = _get("up", build_ffn_up, S2, Bn)
    plep = _get("ple", build_ple, S2)
    cf, cb = mixer_consts()
    onesm = np.ones((128, MC), np.float32)
    cat = lambda lst: np.ascontiguousarray(np.concatenate(lst, 0))
    G = range(NG)
    cs = lambda g: slice(CS * g, CS * (g + 1))

    hsh = [np.ascontiguousarray(np.concatenate([x[b].T[cs(g)] for b in range(Bn)], 1)) for g in G]
    zeros = np.zeros((CS, S2), np.float32)
    onesq = np.ones((NG, S2), np.float32)
    res = run(prep, [{"hT": hsh[g], "mT": zeros, "ssq4": onesq, "wpost": onesm,
                      "wnext": _vm(np.asarray(ln_mix_pre[0])[cs(g)])} for g in G])
    del zeros
    hw = cat([r["hbT"] for r in res])
    ssq = cat([r["ssq"] for r in res])
    lbl = np.asarray(lb_logits, np.float32)
    NJ = (FG + 127) // 128
    for l in range(L):
        win = np.asarray(w_in[l], np.float32)
        lmask = np.zeros((128, 4), np.float32)
        lmask[:, 1:l + 1] = 1.0
        ins = []
        for g in G:
            sl = lambda slot: win[:, slot * 1024 + 128 * g: slot * 1024 + 128 * (g + 1)]
            wfm = np.ascontiguousarray(np.concatenate([sl(0), sl(1), sl(3), sl(4), sl(5), sl(6)], 1))
            wtm = np.ascontiguousarray(np.concatenate([sl(1), sl(2)], 1))
            lg = lbl[:, 128 * g:128 * (g + 1)]
            lbl_fm = np.ascontiguousarray(np.broadcast_to(lg.T[:, None, :], (128, 2, 4)))
            lbl_tm = np.ascontiguousarray(np.broadcast_to(lg[None], (128, 4, 128)))
            hv = np.asarray(hgrn_norm[l], np.float32)[128 * g:128 * (g + 1)]
            hnm = np.ascontiguousarray(np.stack([hv, hv], 1))
            ins.append({"xT": hw, "ssq4": ssq, "wfm": wfm, "wtm": wtm, "lbl_fm": lbl_fm, "lbl_tm": lbl_tm, "lmask": lmask,
                        "hnorm": hnm, "cf": cf, "cb": cb})
        res = run(mixer, ins)
        del ins, hw
        ofull = cat([r["oT"][0:128] for r in res] + [r["oT"][128:256] for r in res])
        del res
        res = run(d2k, [{"xT": ofull, "w": np.ascontiguousarray(np.asarray(w_out[l], np.float32)[:, cs(g)])} for g in G])
        del ofull
        ssq = cat([r["ssq"] for r in res])
        res = run(prep, [{"hT": hsh[g], "mT": res[g]["yT"], "ssq4": ssq, "wpost": _vm(np.asarray(ln_mix_post[l])[cs(g)]),
                          "wnext": _vm(np.asarray(ln_ffn_pre[l])[cs(g)])} for g in G])
        hsh = [r["hnT"] for r in res]
        hw = cat([r["hbT"] for r in res])
        ssq = cat([r["ssq"] for r in res])
        del res
        wup = np.asarray(w_up[l], np.float32)
        cwl = np.asarray(conv_w[l], np.float32)
        cbl = np.asarray(conv_b[l], np.float32)
        ins = []
        for g in G:
            cols = np.concatenate([np.arange(FG * g, FG * (g + 1)), D_FF + np.arange(FG * g, FG * (g + 1))])
            cwp = np.zeros((128, 2 * NJ, 3), np.float32)
            cbp = np.zeros((128, 2 * NJ), np.float32)
            for wh in range(2):
                cc = cols[wh * FG:(wh + 1) * FG]
                for j in range(NJ):
                    cj = cc[j * 128:(j + 1) * 128]
                    cwp[:len(cj), wh * NJ + j, :] = cwl[:, cj].T
                    cbp[:len(cj), wh * NJ + j] = cbl[cj]
            ins.append({"xT": hw, "ssq4": ssq, "w": np.ascontiguousarray(wup[:, cols]), "cw": cwp, "cb": cbp})
        res = run(upp, ins)
        del ins, hw
        gfull = cat([r["gT"] for r in res])
        del res
        res = run(d5k, [{"xT": gfull, "w": np.ascontiguousarray(np.asarray(w_down[l], np.float32)[:, cs(g)])} for g in G])
        del gfull
        ssq = cat([r["ssq"] for r in res])
        res = run(prep, [{"hT": hsh[g], "mT": res[g]["yT"], "ssq4": ssq, "wpost": _vm(np.asarray(ln_ffn_post[l])[cs(g)]),
                          "wnext": onesm} for g in G])
        hsh = [r["hnT"] for r in res]
        hb = cat([r["hbT"] for r in res])
        del res
        nxt = (lambda g: _vm(np.asarray(ln_mix_pre[l + 1])[cs(g)])) if l + 1 < L else (lambda g: onesm)
        pT = np.ascontiguousarray(np.concatenate([np.asarray(p[l][b], np.float32).T for b in range(Bn)], 1))
        res = run(plep, [{"hbT": hb, "hT": hsh[g], "pT": pT,
                          "wpe": np.ascontiguousarray(np.asarray(w_pe[l], np.float32)[:, cs(g)]),
                          "wpg": np.ascontiguousarray(np.asarray(w_pg[l], np.float32)[:, cs(g)]),
                          "wnext": nxt(g)} for g in G])
        del hb, pT
        hsh = [r["hnT"] for r in res]
        hw = cat([r["hwT"] for r in res])
        ssq = cat([r["ssq"] for r in res])
        del res
    out = np.empty((Bn, S, D), np.float32)
    for g in G:
        for b in range(Bn):
            out[b][:, cs(g)] = hsh[g][:, b * S:(b + 1) * S].T
    return out
```
